# Optimizing a Trainium2 kernel written in Bass

```python
import math
import jax, jax.numpy as jnp
from jax import lax
import numpy as np

D_MODEL = 1024
BATCH = 4
SEQ = 8192
DEPTH = 1
DEC_BATCH = 16
DEC_SEQ = 16
PAST_LEN = 2048

CHUNK = 64
D_MIX = D_MODEL
D_ATTN = D_MIX // 2
D_SSM = D_MIX - D_ATTN
HEAD_DIM = 64
N_HEADS = D_ATTN // HEAD_DIM
N_KV_HEADS = 2
Q_PER_KV = N_HEADS // N_KV_HEADS
KV_W = N_KV_HEADS * HEAD_DIM
WINDOW = 128
N_WIN_CHUNKS = WINDOW // CHUNK
ROPE_THETA = 10000.0
SSM_GROUP = 16
N_SSM_GROUPS = D_SSM // SSM_GROUP
SSM_STATE = 64
D_FF = ((8 * D_MODEL // 3 + 127) // 128) * 128
CONV_W = 3
EPS = 1e-6
D_IN = D_ATTN + 2 * KV_W + D_SSM

kernel_name = "hymba_swa_s5_convffn_stream_step"


def rmsnorm(x, g):
    xf = x.astype(jnp.float32)
    y = xf * lax.rsqrt(jnp.mean(xf * xf, axis=-1, keepdims=True) + EPS) * g.astype(jnp.float32)
    return y.astype(x.dtype)


def rope(x, pos):
    half = HEAD_DIM // 2
    inv = ROPE_THETA ** (-jnp.arange(half, dtype=jnp.float32) / half)
    ang = pos.astype(jnp.float32)[:, None] * inv[None, :]
    cos = jnp.cos(ang)[:, None, :]
    sin = jnp.sin(ang)[:, None, :]
    xf = x.astype(jnp.float32)
    x1, x2 = xf[..., :half], xf[..., half:]
    return jnp.concatenate([x1 * cos - x2 * sin, x2 * cos + x1 * sin], axis=-1).astype(x.dtype)


def sink_softmax(s, mask, sink):
    s = jnp.where(mask, s, -jnp.inf)
    m = jnp.maximum(jnp.max(s, axis=-1, keepdims=True), sink)
    e = jnp.exp(s - m)
    return e / (jnp.sum(e, axis=-1, keepdims=True) + jnp.exp(sink - m))


def window_attention_prompt(q, k, v, sinks):
    B, L = q.shape[:2]
    nb = L // CHUNK
    qb = q.reshape(B, nb, CHUNK, N_KV_HEADS, Q_PER_KV, HEAD_DIM)
    pad = N_WIN_CHUNKS * CHUNK

    def band(t):
        tp = jnp.pad(t, ((0, 0), (pad, 0), (0, 0), (0, 0)))
        tp = tp.reshape(B, nb + N_WIN_CHUNKS, CHUNK, N_KV_HEADS, HEAD_DIM)
        return jnp.concatenate([tp[:, j:j + nb] for j in range(N_WIN_CHUNKS + 1)], axis=2)

    kb, vb = band(k), band(v)
    kpos = (jnp.arange(nb)[:, None] - N_WIN_CHUNKS) * CHUNK + jnp.arange((N_WIN_CHUNKS + 1) * CHUNK)[None, :]
    mask = (kpos >= 0)[:, None, None, None, :]
    s = jnp.einsum('bnqgrd,bnkgd->bngrqk', qb, kb).astype(jnp.float32) * (HEAD_DIM ** -0.5)
    sink = sinks.astype(jnp.float32).reshape(N_KV_HEADS, Q_PER_KV)[:, :, None, None]
    p = sink_softmax(s, mask, sink)
    o = jnp.einsum('bngrqk,bnkgd->bnqgrd', p.astype(v.dtype), vb)
    return o.reshape(B, L, D_ATTN)


def window_attention_step(q, k_all, v_all, qpos, kpos, sinks):
    B, S = q.shape[:2]
    qg = q.reshape(B, S, N_KV_HEADS, Q_PER_KV, HEAD_DIM)
    s = jnp.einsum('bqgrd,bkgd->bgrqk', qg, k_all).astype(jnp.float32) * (HEAD_DIM ** -0.5)
    qc = qpos // CHUNK
    kc = kpos // CHUNK
    mask = (kc[None, :] <= qc[:, None]) & (kc[None, :] >= qc[:, None] - N_WIN_CHUNKS) & (kpos[None, :] >= 0)
    sink = sinks.astype(jnp.float32).reshape(N_KV_HEADS, Q_PER_KV)[:, :, None, None]
    p = sink_softmax(s, mask, sink)
    o = jnp.einsum('bgrqk,bkgd->bqgrd', p.astype(v_all.dtype), v_all)
    return o.reshape(B, S, D_ATTN)


def s5_glu(u, h0, A_re, A_im, log_dt, B_re, B_im, C_re, C_im, Dskip, w_glu):
    Bsz, L = u.shape[:2]
    f32 = jnp.float32
    uf = u.astype(f32).reshape(Bsz, L, N_SSM_GROUPS, SSM_GROUP)
    A = lax.complex(A_re.astype(f32), A_im.astype(f32))
    dtA = jnp.exp(log_dt.astype(f32))[:, None] * A
    A_bar = jnp.exp(dtA)
    B_bar = ((A_bar - 1.0) / A)[:, :, None] * lax.complex(B_re.astype(f32), B_im.astype(f32))
    bu = jnp.einsum('blgc,gpc->blgp', uf.astype(jnp.complex64), B_bar)
    if h0 is not None:
        h_prev = lax.complex(h0[0].astype(f32), h0[1].astype(f32))
        bu = bu.at[:, 0].add(A_bar[None] * h_prev)
    a = jnp.broadcast_to(A_bar, (1, L, N_SSM_GROUPS, SSM_STATE))

    def combine(e1, e2):
        a1, b1 = e1
        a2, b2 = e2
        return a1 * a2, a2 * b1 + b2

    _, h = lax.associative_scan(combine, (a, bu), axis=1)
    C = lax.complex(C_re.astype(f32), C_im.astype(f32))
    y = jnp.einsum('gcp,blgp->blgc', C, h).real + Dskip.astype(f32)[None, None] * uf
    z = jax.nn.gelu(y.reshape(Bsz, L, D_SSM))
    out = z * jax.nn.sigmoid(z @ w_glu.astype(f32))
    h_last = h[:, -1]
    return out.astype(u.dtype), h_last.real, h_last.imag


def conv_ffn(x, prev, w_up, conv_w, conv_b, w_down):
    B, L = x.shape[:2]
    up = x @ w_up
    if prev is None:
        prev = jnp.zeros((B, CONV_W - 1, up.shape[-1]), up.dtype)
    up_p = jnp.concatenate([prev.astype(up.dtype), up], axis=1)
    c = sum(up_p[:, j:j + L] * conv_w[j] for j in range(CONV_W)) + conv_b
    gate, val = jnp.split(c, 2, axis=-1)
    return (jax.nn.silu(gate) * val) @ w_down, up_p[:, -(CONV_W - 1):]


def layer(x, start, lw, kv_prev, ssm_prev, conv_prev):
    B, L = x.shape[:2]
    pos = start + jnp.arange(L)
    h = rmsnorm(x, lw['norm1'])
    proj = h @ lw['w_in']
    q, k, v, u = jnp.split(proj, [D_ATTN, D_ATTN + KV_W, D_ATTN + 2 * KV_W], axis=-1)
    q = rope(q.reshape(B, L, N_HEADS, HEAD_DIM), pos)
    k = rope(k.reshape(B, L, N_KV_HEADS, HEAD_DIM), pos)
    v = v.reshape(B, L, N_KV_HEADS, HEAD_DIM)
    if kv_prev is None:
        a = window_attention_prompt(q, k, v, lw['sinks'])
        new_k, new_v = k[:, -WINDOW:], v[:, -WINDOW:]
    else:
        ck, cv = kv_prev
        n_buf = ck.shape[1]
        k_all = jnp.concatenate([ck.astype(k.dtype), k], axis=1)
        v_all = jnp.concatenate([cv.astype(v.dtype), v], axis=1)
        kpos = start - n_buf + jnp.arange(n_buf + L)
        a = window_attention_step(q, k_all, v_all, pos, kpos, lw['sinks'])
        new_k, new_v = k_all[:, -n_buf:], v_all[:, -n_buf:]
    s, h_re, h_im = s5_glu(u, ssm_prev, lw['A_re'], lw['A_im'], lw['log_dt'], lw['B_re'], lw['B_im'],
                           lw['C_re'], lw['C_im'], lw['D'], lw['w_glu'])
    merged = jnp.concatenate([rmsnorm(a, lw['onorm_a']), rmsnorm(s, lw['onorm_s'])], axis=-1)
    x = x + merged @ lw['w_out']
    f, new_conv = conv_ffn(rmsnorm(x, lw['norm2']), conv_prev, lw['w_up'], lw['conv_w'], lw['conv_b'], lw['w_down'])
    x = x + f
    return x, new_k, new_v, h_re, h_im, new_conv


def setup_inputs(seed: int = 0) -> dict:
    key = jax.random.key(seed)
    ks = jax.random.split(key, 32)
    f32 = jnp.float32
    n = lambda k, shape, scale: jax.random.normal(k, shape, f32) * scale
    kv_rows = min(WINDOW, PAST_LEN)
    log_dt = jax.random.uniform(ks[10], (DEPTH, N_SSM_GROUPS), f32, math.log(1e-3), math.log(1e-1))
    a_im = jnp.broadcast_to(math.pi * jnp.arange(SSM_STATE, dtype=f32), (DEPTH, N_SSM_GROUPS, SSM_STATE))
    return {
        "x_prompt": n(ks[0], (BATCH, SEQ, D_MODEL), 1.0),
        "x_sample": n(ks[1], (DEC_BATCH, DEC_SEQ, D_MODEL), 1.0),
        "cache_k": n(ks[2], (DEPTH, DEC_BATCH, kv_rows, N_KV_HEADS, HEAD_DIM), 1.0),
        "cache_v": n(ks[3], (DEPTH, DEC_BATCH, kv_rows, N_KV_HEADS, HEAD_DIM), 1.0),
        "state_ssm_re": n(ks[4], (DEPTH, DEC_BATCH, N_SSM_GROUPS, SSM_STATE), 0.5),
        "state_ssm_im": n(ks[5], (DEPTH, DEC_BATCH, N_SSM_GROUPS, SSM_STATE), 0.5),
        "state_conv": n(ks[6], (DEPTH, DEC_BATCH, CONV_W - 1, 2 * D_FF), 1.0),
        "norm1_g": 1.0 + n(ks[7], (DEPTH, D_MODEL), 0.02),
        "w_in": n(ks[8], (DEPTH, D_MODEL, D_IN), D_MODEL ** -0.5),
        "attn_sinks": n(ks[9], (DEPTH, N_HEADS), 0.5),
        "ssm_A_re": -0.5 + n(ks[11], (DEPTH, N_SSM_GROUPS, SSM_STATE), 0.01),
        "ssm_A_im": a_im + n(ks[12], (DEPTH, N_SSM_GROUPS, SSM_STATE), 0.01),
        "ssm_log_dt": log_dt,
        "ssm_B_re": n(ks[13], (DEPTH, N_SSM_GROUPS, SSM_STATE, SSM_GROUP), (2 * SSM_GROUP) ** -0.5),
        "ssm_B_im": n(ks[14], (DEPTH, N_SSM_GROUPS, SSM_STATE, SSM_GROUP), (2 * SSM_GROUP) ** -0.5),
        "ssm_C_re": n(ks[15], (DEPTH, N_SSM_GROUPS, SSM_GROUP, SSM_STATE), (2 * SSM_STATE) ** -0.5),
        "ssm_C_im": n(ks[16], (DEPTH, N_SSM_GROUPS, SSM_GROUP, SSM_STATE), (2 * SSM_STATE) ** -0.5),
        "ssm_D": n(ks[17], (DEPTH, N_SSM_GROUPS, SSM_GROUP), 1.0),
        "w_glu": n(ks[18], (DEPTH, D_SSM, D_SSM), D_SSM ** -0.5),
        "onorm_attn_g": 1.0 + n(ks[19], (DEPTH, D_ATTN), 0.02),
        "onorm_ssm_g": 1.0 + n(ks[20], (DEPTH, D_SSM), 0.02),
        "w_out": n(ks[21], (DEPTH, D_MIX, D_MODEL), D_MIX ** -0.5),
        "norm2_g": 1.0 + n(ks[22], (DEPTH, D_MODEL), 0.02),
        "w_up": n(ks[23], (DEPTH, D_MODEL, 2 * D_FF), D_MODEL ** -0.5),
        "conv_w": n(ks[24], (DEPTH, CONV_W, 2 * D_FF), CONV_W ** -0.5),
        "conv_b": n(ks[25], (DEPTH, 2 * D_FF), 0.01),
        "w_down": n(ks[26], (DEPTH, D_FF, D_MODEL), D_FF ** -0.5),
        "final_g": 1.0 + n(ks[27], (D_MODEL,), 0.02),
    }


def reference(x_prompt, x_sample, cache_k, cache_v, state_ssm_re, state_ssm_im, state_conv,
              norm1_g, w_in, attn_sinks, ssm_A_re, ssm_A_im, ssm_log_dt, ssm_B_re, ssm_B_im,
              ssm_C_re, ssm_C_im, ssm_D, w_glu, onorm_attn_g, onorm_ssm_g, w_out, norm2_g,
              w_up, conv_w, conv_b, w_down, final_g):
    xp, xs = x_prompt, x_sample
    kp_l, vp_l, rp_l, ip_l, cp_l = [], [], [], [], []
    ks_l, vs_l, rs_l, is_l, cs_l = [], [], [], [], []
    for l in range(DEPTH):
        lw = dict(norm1=norm1_g[l], w_in=w_in[l], sinks=attn_sinks[l], A_re=ssm_A_re[l], A_im=ssm_A_im[l],
                  log_dt=ssm_log_dt[l], B_re=ssm_B_re[l], B_im=ssm_B_im[l], C_re=ssm_C_re[l],
                  C_im=ssm_C_im[l], D=ssm_D[l], w_glu=w_glu[l], onorm_a=onorm_attn_g[l],
                  onorm_s=onorm_ssm_g[l], w_out=w_out[l], norm2=norm2_g[l], w_up=w_up[l],
                  conv_w=conv_w[l], conv_b=conv_b[l], w_down=w_down[l])
        xp, kp, vp, rp, ip, cp = layer(xp, 0, lw, None, None, None)
        xs, ks_, vs_, rs_, is_, cs_ = layer(xs, PAST_LEN, lw, (cache_k[l], cache_v[l]),
                                            (state_ssm_re[l], state_ssm_im[l]), state_conv[l])
        kp_l.append(kp); vp_l.append(vp); rp_l.append(rp); ip_l.append(ip); cp_l.append(cp)
        ks_l.append(ks_); vs_l.append(vs_); rs_l.append(rs_); is_l.append(is_); cs_l.append(cs_)
    y_prompt = rmsnorm(xp, final_g)
    y_sample = rmsnorm(xs, final_g)
    return (y_prompt, y_sample,
            jnp.stack(kp_l), jnp.stack(vp_l), jnp.stack(rp_l), jnp.stack(ip_l), jnp.stack(cp_l),
            jnp.stack(ks_l), jnp.stack(vs_l), jnp.stack(rs_l), jnp.stack(is_l), jnp.stack(cs_l))
```

```python
import contextlib
import os
KSTOP = int(os.environ.get('KSTOP', '99'))
KS2 = int(os.environ.get('KS2', '99'))
import math
import numpy as np
import concourse.bass as bass
import concourse.mybir as mybir
from concourse.bass_utils import run_bass_kernel_spmd

F32 = mybir.dt.float32
BF16 = mybir.dt.bfloat16
ALU = mybir.AluOpType
AF = mybir.ActivationFunctionType
AX = mybir.AxisListType

D = 1024
T = 256
NHID = 22
EPS = 1e-6
SEG = 256
NEG = -30000.0
TWO_PI = 2.0 * math.pi


class Tk:
    def __init__(self, name):
        self.name = name
        self.w = None
        self.r = {}
        self.dsem = None
        self.dcnt = 0


class B:
    def __init__(self, t, name):
        self.t = t
        self.k = Tk(name)

    def __getitem__(self, key):
        return self.t[key]


class Ctx:
    def __init__(self, nc, es):
        self.nc = nc
        self.es = es
        self.e = dict(pe=nc.tensor, act=nc.scalar, dve=nc.vector, pool=nc.gpsimd, sp=nc.sync)
        self.sem = {k: es.enter_context(nc.semaphore("s_" + k)) for k in ["pe", "act", "dve", "pool"]}
        self.cnt = {k: 0 for k in self.sem}
        self.waited = {k: {} for k in self.e}
        self.pend = {k: ([], []) for k in self.e}
        self.nds = 0
        self.nt = 0
        self.cur_es = es
        self.dma_owners = []

    def barrier(self):
        for e in self.e:
            for k in self.sem:
                if self.cnt[k] > 0:
                    self.e[e].wait_ge(self.sem[k], self.cnt[k])
                    self.waited[e][k] = self.cnt[k]
            for tk in self.dma_owners:
                self.e[e].wait_ge(tk.dsem, tk.dcnt)

    def sb(self, shape, dt, name=None):
        self.nt += 1
        name = "sb_" + (name or f"t{self.nt}")
        return B(self.cur_es.enter_context(self.nc.sbuf_tensor(name, list(shape), dt)), name)

    def ps(self, name):
        return B(self.es.enter_context(self.nc.psum_tensor(name, [128, 512], F32)), name)

    def _deps(self, eng, reads, writes):
        deps = {}

        def add(d, same_ok):
            if d is None:
                return
            key, h, v = d
            if key == eng and not same_ok and eng == "pe":
                return
            if key not in deps or deps[key][1] < v:
                deps[key] = (h, v)
        for b in reads:
            add(b.k.w, True)
        for b in writes:
            add(b.k.w, False)
            for d in b.k.r.values():
                add(d, False)
        for key, (h, v) in deps.items():
            if self.waited[eng].get(key, 0) >= v:
                continue
            self.e[eng].wait_ge(h, v)
            self.waited[eng][key] = v

    def op(self, eng, fn, r=(), w=(), inc=True):
        self._deps(eng, r, w)
        ins = fn(self.e[eng])
        pr, pw = self.pend[eng]
        pr.extend(r)
        pw.extend(w)
        if inc:
            self.cnt[eng] += 1
            v = self.cnt[eng]
            ins.then_inc(self.sem[eng], 1)
            d = (eng, self.sem[eng], v)
            for b in pr:
                b.k.r[eng] = d
            for b in pw:
                b.k.w = d
                b.k.r = {}
            self.pend[eng] = ([], [])
        return ins

    def dma(self, out, in_, r=(), w=(), owner=None, slow=False):
        self._deps("sp", r, w)
        ow = (owner or (w[0] if w else r[0])).k
        if ow.dsem is None:
            ow.dsem = self.es.enter_context(self.nc.semaphore(f"d{self.nds}"))
            self.nds += 1
            self.dma_owners.append(ow)
        ow.dcnt += 16
        if slow:
            ins = self.nc.sync.dma_start(out=out, in_=in_, allow_slow_non_contiguous=True)
        else:
            ins = self.nc.sync.dma_start(out=out, in_=in_)
        ins.then_inc(ow.dsem, 16)
        d = ("D" + ow.name, ow.dsem, ow.dcnt)
        for b in r:
            b.k.r[d[0]] = d
        for b in w:
            b.k.w = d
            b.k.r = {}
        return ow


class DR:
    def __init__(self, ap, name):
        self.ap = ap
        self.k = Tk(name)


def build(n_pre, n_main, do_sample=True):
    nc = bass.Bass("TRN2", target_bir_lowering=False)
    NP = n_pre * T
    NM = n_main * T

    def din(name, shape):
        return nc.dram_tensor(name, list(shape), F32, kind="ExternalInput").ap()

    def dout(name, shape):
        return nc.dram_tensor(name, list(shape), F32, kind="ExternalOutput").ap()

    xprev = din("xprev", [NP + T, D])
    xmain = din("xmain", [NM, D])
    xs = din("xs", [2, 16, D])
    cache_k = din("cache_k", [2, 128, 128])
    cache_v = din("cache_v", [2, 128, 128])
    st_re = din("st_re", [2, 32, 64])
    st_im = din("st_im", [2, 32, 64])
    st_conv = din("st_conv", [2, 2, 5632])
    norm1_g = din("norm1_g", [D])
    w_in = din("w_in", [D, 1280])
    sinks = din("sinks", [8])
    A_re = din("A_re", [32, 64])
    A_im = din("A_im", [32, 64])
    log_dt = din("log_dt", [32])
    B_re = din("B_re", [32, 64, 16])
    B_im = din("B_im", [32, 64, 16])
    C_re = din("C_re", [32, 16, 64])
    C_im = din("C_im", [32, 16, 64])
    ssm_D = din("ssm_D", [32, 16])
    w_glu = din("w_glu", [512, 512])
    on_a = din("on_a", [512])
    on_s = din("on_s", [512])
    w_out = din("w_out", [D, D])
    norm2_g = din("norm2_g", [D])
    w_up = din("w_up", [D, 5632])
    conv_w = din("conv_w", [3, 5632])
    conv_b = din("conv_b", [5632])
    w_down = din("w_down", [2816, D])
    final_g = din("final_g", [D])
    ident_in = din("ident", [128, 128])
    maskc_in = din("maskc", [128, 5])
    ropec_in = din("ropec", [64, T + NM + 256])
    ropes_in = din("ropes", [64, T + NM + 256])

    y_main = dout("y_main", [NM, D])
    y_s = dout("y_s", [2, 16, D])
    k_out = dout("k_out", [128, 128])
    v_out = dout("v_out", [128, 128])
    sre_out = dout("sre_out", [32, 64])
    sim_out = dout("sim_out", [32, 64])
    conv_out = dout("conv_out", [2, 5632])
    ks_out = dout("ks_out", [2, 128, 128])
    vs_out = dout("vs_out", [2, 128, 128])
    sres_out = dout("sres_out", [2, 32, 64])
    sims_out = dout("sims_out", [2, 32, 64])
    convs_out = dout("convs_out", [2, 2, 5632])

    wup_scr = DR(nc.dram_tensor("wup_scr", [NHID, 128, 8 * 256], BF16, kind="Internal").ap(), "wup_scr")
    wdn_scr = DR(nc.dram_tensor("wdn_scr", [NHID, 128, 1024], BF16, kind="Internal").ap(), "wdn_scr")

    with contextlib.ExitStack() as es:
        cx = Ctx(nc, es)
        sb = cx.sb
        op = cx.op
        dma = cx.dma
        out_owners = []

        banks = [cx.ps(f"ps{i}") for i in range(8)]
        bank_i = [0]

        def nb():
            b = banks[bank_i[0] % 8]
            bank_i[0] += 1
            return b

        ident_f = sb([128, 128], F32, "ident_f")
        ident_b = sb([128, 128], BF16, "ident_b")
        maskc = sb([128, 5], F32, "maskc")
        pi_c = sb([128, 1], F32, "pi_c")
        eps_c = sb([128, 1], F32, "eps_c")
        eps4_c = sb([128, 1], F32, "eps4_c")
        ones_b = sb([128, 128], BF16, "ones_b")
        g1c = sb([128, 8], F32, "g1c")
        g2c = sb([128, 8], F32, "g2c")
        goc = sb([128, 8], F32, "goc")
        GF = sb([128, D], F32, "GF")
        ES = sb([128, 8], F32, "ES")
        cw = sb([128, 3, 44], F32, "cw")
        cb = sb([128, 44], F32, "cb")
        Dcol = sb([128, 4], F32, "Dcol")
        Win = sb([128, 8, 1920], BF16, "Win")
        Wout = sb([128, 8, 1024], BF16, "Wout")
        Wglu = sb([128, 4, 512], BF16, "Wglu")
        Bz = sb([128, 16, 2, 128], BF16, "Bz")
        Cz = sb([128, 16, 2, 32], BF16, "Cz")
        Ec = sb([128, 16, SEG], F32, "Ec")
        Es = sb([128, 16, SEG], F32, "Es")
        rmag = sb([128, 16], F32, "rmag")
        TAIL = sb([128, 44, 2], F32, "TAIL")
        Hc = sb([128, 2, 16], F32, "Hc")
        es2 = contextlib.ExitStack()
        cx.cur_es = es2
        NSTG = 8
        stage = [sb([128, 1280], F32, f"stage{i}") for i in range(NSTG)]
        stb = [sb([128, 1024], BF16, f"stb{i}") for i in range(NSTG)]

        dma(ident_f[:], ident_in, w=[ident_f])
        dma(maskc[:], maskc_in, w=[maskc])
        dma(g1c[:], norm1_g.rearrange("(k p) -> p k", p=128), w=[g1c], slow=True)
        dma(g2c[:], norm2_g.rearrange("(k p) -> p k", p=128), w=[g2c], slow=True)
        dma(goc[:, 0:4], on_a.rearrange("(k p) -> p k", p=128), w=[goc], slow=True)
        dma(goc[:, 4:8], on_s.rearrange("(k p) -> p k", p=128), w=[goc], slow=True)
        dma(GF[:], final_g.partition_broadcast(128), w=[GF])
        dma(ES[:], sinks.partition_broadcast(128), w=[ES])
        for j in range(3):
            dma(cw[:, j, :], conv_w[j].rearrange("(c p) -> p c", p=128), w=[cw], slow=True)
        dma(cb[:], conv_b.rearrange("(c p) -> p c", p=128), w=[cb], slow=True)
        dma(Dcol[:], ssm_D.rearrange("(k g) c -> (g c) k", g=8), w=[Dcol], slow=True)

        op("pool", lambda e: e.memset(pi_c[:], math.pi), w=[pi_c])
        op("pool", lambda e: e.memset(eps_c[:], EPS), w=[eps_c])
        op("pool", lambda e: e.memset(eps4_c[:], 4 * EPS), w=[eps4_c])
        op("pool", lambda e: e.memset(ones_b[:], 1.0), w=[ones_b])
        op("pool", lambda e: e.memset(TAIL[:], 0.0), w=[TAIL])
        op("pool", lambda e: e.memset(Hc[:], 0.0), w=[Hc])
        op("dve", lambda e: e.tensor_copy(out=ident_b[:], in_=ident_f[:]), r=[ident_f], w=[ident_b])
        op("act", lambda e: e.activation(out=ES[:], in_=ES[:], func=AF.Exp), r=[ES], w=[ES])

        def small(name, shape=(128, 16)):
            return sb(list(shape), F32, name)
        Are = small("Are"); Aim = small("Aim"); Ldt = small("Ldt")
        dma(Are[:], A_re.rearrange("(q gl) p -> (gl p) q", gl=2), w=[Are], slow=True)
        dma(Aim[:], A_im.rearrange("(q gl) p -> (gl p) q", gl=2), w=[Aim], slow=True)
        ldv = log_dt.rearrange("(q gl) -> gl q", gl=2)
        for gl in range(2):
            dma(Ldt[gl * 64:(gl + 1) * 64, :], ldv[gl].partition_broadcast(64), w=[Ldt], slow=True)
        Bre = sb([128, 16, 16], F32, "Bre"); Bim = sb([128, 16, 16], F32, "Bim")
        dma(Bre[:], B_re.rearrange("(q gl) p c -> (gl p) q c", gl=2), w=[Bre])
        dma(Bim[:], B_im.rearrange("(q gl) p c -> (gl p) q c", gl=2), w=[Bim])
        Cn = [sb([16, 128], F32, "Cn0"), sb([16, 128], F32, "Cn1")]
        Cv = [Csrc.rearrange("(q gl) c p -> q c gl p", gl=2) for Csrc in (C_re, C_im)]

        dt_ = small("dt_"); ar = small("ar"); ai = small("ai"); xx = small("xx"); xc = small("xc")
        cth = small("cth"); sth = small("sth"); lre = small("lre"); lim = small("lim")
        t1 = small("t1"); t2 = small("t2"); den = small("den"); kre = small("kre"); kim = small("kim")
        V = "dve"
        op("act", lambda e: e.activation(out=dt_[:], in_=Ldt[:], func=AF.Exp), r=[Ldt], w=[dt_])
        op(V, lambda e: e.tensor_tensor(out=ar[:], in0=dt_[:], in1=Are[:], op=ALU.mult), r=[dt_, Are], w=[ar])
        op(V, lambda e: e.tensor_tensor(out=ai[:], in0=dt_[:], in1=Aim[:], op=ALU.mult), r=[dt_, Aim], w=[ai])
        op("act", lambda e: e.activation(out=rmag[:], in_=ar[:], func=AF.Exp), r=[ar], w=[rmag])
        ti_ = sb([128, 16], mybir.dt.int32, "ti_")
        tf_ = small("tf_")

        def reduce_2pi(dst, src, shift):
            op(V, lambda e: e.tensor_scalar(out=dst[:], in0=src[:], scalar1=shift, scalar2=None, op0=ALU.add), r=[src], w=[dst])
            op(V, lambda e: e.tensor_scalar(out=ti_[:], in0=dst[:], scalar1=1.0 / TWO_PI, scalar2=None, op0=ALU.mult), r=[dst], w=[ti_])
            op(V, lambda e: e.tensor_copy(out=tf_[:], in_=ti_[:]), r=[ti_], w=[tf_])
            op(V, lambda e: e.scalar_tensor_tensor(out=dst[:], in0=tf_[:], scalar=-TWO_PI, in1=dst[:], op0=ALU.mult, op1=ALU.add), r=[tf_, dst], w=[dst])
            op(V, lambda e: e.tensor_scalar(out=tf_[:], in0=dst[:], scalar1=0.0, scalar2=None, op0=ALU.is_lt), r=[dst], w=[tf_])
            op(V, lambda e: e.scalar_tensor_tensor(out=dst[:], in0=tf_[:], scalar=TWO_PI, in1=dst[:], op0=ALU.mult, op1=ALU.add), r=[tf_, dst], w=[dst])
        reduce_2pi(xx, ai, TWO_PI)
        reduce_2pi(xc, xx, 0.5 * math.pi)
        op("act", lambda e: e.activation(out=sth[:], in_=xx[:], func=AF.Sin, bias=pi_c[:], scale=-1.0), r=[xx, pi_c], w=[sth])
        op("act", lambda e: e.activation(out=cth[:], in_=xc[:], func=AF.Sin, bias=pi_c[:], scale=-1.0), r=[xc, pi_c], w=[cth])
        op(V, lambda e: e.tensor_tensor(out=lre[:], in0=rmag[:], in1=cth[:], op=ALU.mult), r=[rmag, cth], w=[lre])
        op(V, lambda e: e.tensor_tensor(out=lim[:], in0=rmag[:], in1=sth[:], op=ALU.mult), r=[rmag, sth], w=[lim])
        op(V, lambda e: e.tensor_scalar(out=t1[:], in0=lre[:], scalar1=-1.0, scalar2=None, op0=ALU.add), r=[lre], w=[t1])
        op(V, lambda e: e.tensor_tensor(out=den[:], in0=Are[:], in1=Are[:], op=ALU.mult), r=[Are], w=[den])
        op(V, lambda e: e.tensor_tensor(out=t2[:], in0=Aim[:], in1=Aim[:], op=ALU.mult), r=[Aim], w=[t2])
        op(V, lambda e: e.tensor_tensor(out=den[:], in0=den[:], in1=t2[:], op=ALU.add), r=[den, t2], w=[den])
        op(V, lambda e: e.reciprocal(out=den[:], in_=den[:]), r=[den], w=[den])
        op(V, lambda e: e.tensor_tensor(out=kre[:], in0=t1[:], in1=Are[:], op=ALU.mult), r=[t1, Are], w=[kre])
        op(V, lambda e: e.tensor_tensor(out=t2[:], in0=lim[:], in1=Aim[:], op=ALU.mult), r=[lim, Aim], w=[t2])
        op(V, lambda e: e.tensor_tensor(out=kre[:], in0=kre[:], in1=t2[:], op=ALU.add), r=[kre, t2], w=[kre])
        op(V, lambda e: e.tensor_tensor(out=kre[:], in0=kre[:], in1=den[:], op=ALU.mult), r=[kre, den], w=[kre])
        op(V, lambda e: e.tensor_tensor(out=kim[:], in0=lim[:], in1=Are[:], op=ALU.mult), r=[lim, Are], w=[kim])
        op(V, lambda e: e.tensor_tensor(out=t2[:], in0=t1[:], in1=Aim[:], op=ALU.mult), r=[t1, Aim], w=[t2])
        op(V, lambda e: e.tensor_tensor(out=kim[:], in0=kim[:], in1=t2[:], op=ALU.subtract), r=[kim, t2], w=[kim])
        op(V, lambda e: e.tensor_tensor(out=kim[:], in0=kim[:], in1=den[:], op=ALU.mult), r=[kim, den], w=[kim])
        Bbr = sb([128, 16, 16], F32, "Bbr"); Bbi = sb([128, 16, 16], F32, "Bbi"); Btm = sb([128, 16, 16], F32, "Btm")
        kreb = kre[:, :, None].broadcast_to([128, 16, 16])
        kimb = kim[:, :, None].broadcast_to([128, 16, 16])
        op(V, lambda e: e.tensor_tensor(out=Bbr[:], in0=Bre[:], in1=kreb, op=ALU.mult), r=[Bre, kre], w=[Bbr])
        op(V, lambda e: e.tensor_tensor(out=Btm[:], in0=Bim[:], in1=kimb, op=ALU.mult), r=[Bim, kim], w=[Btm])
        op(V, lambda e: e.tensor_tensor(out=Bbr[:], in0=Bbr[:], in1=Btm[:], op=ALU.subtract), r=[Bbr, Btm], w=[Bbr])
        op(V, lambda e: e.tensor_tensor(out=Bbi[:], in0=Bim[:], in1=kreb, op=ALU.mult), r=[Bim, kre], w=[Bbi])
        op(V, lambda e: e.tensor_tensor(out=Btm[:], in0=Bre[:], in1=kimb, op=ALU.mult), r=[Bre, kim], w=[Btm])
        op(V, lambda e: e.tensor_tensor(out=Bbi[:], in0=Bbi[:], in1=Btm[:], op=ALU.add), r=[Bbi, Btm], w=[Bbi])
        Zq = [sb([128, 128], F32, "Zq0"), sb([128, 128], F32, "Zq1")]
        zi = 0
        for q in range(16):
            qm = q % 4
            for ri, Bb in enumerate((Bbr, Bbi)):
                z = Zq[zi % 2]
                zi += 1
                op("pool", lambda e: e.memset(z[:], 0.0), w=[z])
                op("pool", lambda e: e.tensor_copy(out=z[0:64, 32 * qm:32 * qm + 16], in_=Bb[0:64, q, :]), r=[Bb], w=[z])
                op("pool", lambda e: e.tensor_copy(out=z[64:128, 32 * qm + 16:32 * qm + 32], in_=Bb[64:128, q, :]), r=[Bb, z], w=[z])
                pb = nb()
                op("pe", lambda e: e.transpose(out=pb[:, 0:128], in_=z[:], identity=ident_f[:]), r=[z, ident_f], w=[pb])
                op("act", lambda e: e.copy(out=Bz[:, q, ri, :], in_=pb[:, 0:128]), r=[pb], w=[Bz])
        op("pool", lambda e: e.memset(Cz[:], 0.0), w=[Cz])
        for q in range(16):
            for ri in range(2):
                pb = nb()
                dma(Cn[ri][:].rearrange("c (gl p) -> c gl p", gl=2), Cv[ri][q], w=[Cn[ri]])
                op("pe", lambda e: e.transpose(out=pb[:, 0:16], in_=Cn[ri][:, :], identity=ident_f[0:16, 0:16]), r=[Cn[ri], ident_f], w=[pb])
                sc = 1.0 if ri == 0 else -1.0
                op("act", lambda e: e.mul(out=Cz[0:64, q, ri, 0:16], in_=pb[0:64, 0:16], mul=sc), r=[pb], w=[Cz])
                op("act", lambda e: e.mul(out=Cz[64:128, q, ri, 16:32], in_=pb[64:128, 0:16], mul=sc), r=[pb, Cz], w=[Cz])
        op("pool", lambda e: e.memset(Ec[:, :, 0:1], 1.0), w=[Ec])
        op("pool", lambda e: e.memset(Es[:, :, 0:1], 0.0), w=[Es])
        op("pool", lambda e: e.tensor_copy(out=Ec[:, :, 1], in_=cth[:]), r=[cth, Ec], w=[Ec])
        op("pool", lambda e: e.tensor_copy(out=Es[:, :, 1], in_=sth[:]), r=[sth, Es], w=[Es])
        ta = sb([128, 16, SEG // 2], F32, "ta"); tb_ = sb([128, 16, SEG // 2], F32, "tb_")
        n = 1
        while n < SEG:
            if n > 1:
                a, b_, c, d_ = Ec[:, :, n - 1], Es[:, :, n - 1], Ec[:, :, 1], Es[:, :, 1]
                op(V, lambda e: e.tensor_tensor(out=ta[:, :, 0], in0=a, in1=c, op=ALU.mult), r=[Ec], w=[ta])
                op(V, lambda e: e.tensor_tensor(out=tb_[:, :, 0], in0=b_, in1=d_, op=ALU.mult), r=[Es], w=[tb_])
                op(V, lambda e: e.tensor_tensor(out=Ec[:, :, n], in0=ta[:, :, 0], in1=tb_[:, :, 0], op=ALU.subtract), r=[ta, tb_, Ec], w=[Ec])
                op(V, lambda e: e.tensor_tensor(out=ta[:, :, 0], in0=a, in1=d_, op=ALU.mult), r=[Ec, Es], w=[ta])
                op(V, lambda e: e.tensor_tensor(out=tb_[:, :, 0], in0=b_, in1=c, op=ALU.mult), r=[Es, Ec], w=[tb_])
                op(V, lambda e: e.tensor_tensor(out=Es[:, :, n], in0=ta[:, :, 0], in1=tb_[:, :, 0], op=ALU.add), r=[ta, tb_, Es], w=[Es])
            if n > 1:
                m = n - 1
                cn = Ec[:, :, n:n + 1].broadcast_to([128, 16, m])
                sn = Es[:, :, n:n + 1].broadcast_to([128, 16, m])
                op(V, lambda e: e.tensor_tensor(out=ta[:, :, 0:m], in0=Ec[:, :, 1:n], in1=cn, op=ALU.mult), r=[Ec], w=[ta])
                op(V, lambda e: e.tensor_tensor(out=tb_[:, :, 0:m], in0=Es[:, :, 1:n], in1=sn, op=ALU.mult), r=[Es], w=[tb_])
                op(V, lambda e: e.tensor_tensor(out=Ec[:, :, n + 1:2 * n], in0=ta[:, :, 0:m], in1=tb_[:, :, 0:m], op=ALU.subtract), r=[ta, tb_, Ec], w=[Ec])
                op(V, lambda e: e.tensor_tensor(out=ta[:, :, 0:m], in0=Ec[:, :, 1:n], in1=sn, op=ALU.mult), r=[Ec, Es], w=[ta])
                op(V, lambda e: e.tensor_tensor(out=tb_[:, :, 0:m], in0=Es[:, :, 1:n], in1=cn, op=ALU.mult), r=[Es, Ec], w=[tb_])
                op(V, lambda e: e.tensor_tensor(out=Es[:, :, n + 1:2 * n], in0=ta[:, :, 0:m], in1=tb_[:, :, 0:m], op=ALU.add), r=[ta, tb_, Es], w=[Es])
            n *= 2

        for kt in range(8):
            st = stage[kt % NSTG]
            dma(st[:], w_in[kt * 128:(kt + 1) * 128, :], w=[st])
            g = g1c[:, kt:kt + 1]
            eng = "dve"
            op(eng, lambda e: e.tensor_scalar(out=Win[:, kt, 0:512], in0=st[:, 0:512], scalar1=g, scalar2=None, op0=ALU.mult),
               r=[st, g1c], w=[Win])
            op(eng, lambda e: e.tensor_scalar(out=Win[:, kt, 1024:1152], in0=st[:, 512:640], scalar1=g, scalar2=None, op0=ALU.mult),
               r=[st, g1c], w=[Win])
            op(eng, lambda e: e.tensor_scalar(out=Win[:, kt, 1280:1792], in0=st[:, 768:1280], scalar1=g, scalar2=None, op0=ALU.mult),
               r=[st, g1c], w=[Win])
            op(eng, lambda e: e.tensor_scalar(out=Win[:, kt, 1792:1920], in0=st[:, 640:768], scalar1=g, scalar2=None, op0=ALU.mult),
               r=[st, g1c], w=[Win])
            for (dst0, src0, nh) in ((512, 0, 8), (1152, 512, 2)):
                dv = Win[:, kt, dst0:dst0 + nh * 64].rearrange("p (h two d) -> p h two d", two=2, d=32)
                sv = st[:, src0:src0 + nh * 64].rearrange("p (h two d) -> p h two d", two=2, d=32)
                op(eng, lambda e: e.tensor_scalar(out=dv[:, :, 0, :], in0=sv[:, :, 1, :], scalar1=g, scalar2=-1.0,
                                                  op0=ALU.mult, op1=ALU.mult), r=[st, g1c], w=[Win])
                op(eng, lambda e: e.tensor_scalar(out=dv[:, :, 1, :], in0=sv[:, :, 0, :], scalar1=g, scalar2=None,
                                                  op0=ALU.mult), r=[st, g1c], w=[Win])
        for kt in range(8):
            st = stage[kt % NSTG]
            dma(st[:, 0:1024], w_out[kt * 128:(kt + 1) * 128, :], w=[st])
            op("dve",
               lambda e: e.tensor_scalar(out=Wout[:, kt, :], in0=st[:, 0:1024], scalar1=goc[:, kt:kt + 1], scalar2=None, op0=ALU.mult),
               r=[st, goc], w=[Wout])
        for kt in range(4):
            st = stage[(kt + 4) % NSTG]
            dma(st[:, 0:512], w_glu[kt * 128:(kt + 1) * 128, :], w=[st])
            op("act", lambda e: e.copy(out=Wglu[:, kt, :], in_=st[:, 0:512]), r=[st], w=[Wglu])
        wup_v = wup_scr.ap.rearrange("j p (k gv c) -> j p k gv c", k=8, gv=2)
        it = 0
        for kt in range(8):
            for gv in range(2):
                for cblk in range(3):
                    c0 = cblk * 1024
                    ncol = min(1024, 2816 - c0)
                    st = stage[it % NSTG]
                    sbf = stb[it % NSTG]
                    dma(st[:, 0:ncol], w_up[kt * 128:(kt + 1) * 128, gv * 2816 + c0: gv * 2816 + c0 + ncol], w=[st])
                    op("dve",
                       lambda e: e.tensor_scalar(out=sbf[:, 0:ncol], in0=st[:, 0:ncol], scalar1=g2c[:, kt:kt + 1], scalar2=None, op0=ALU.mult),
                       r=[st, g2c], w=[sbf])
                    j0 = c0 // 128
                    nj = ncol // 128
                    dma(wup_v[j0:j0 + nj, :, kt, gv, :].rearrange("j p c -> p j c"),
                        sbf[:, 0:ncol].rearrange("p (j c) -> p j c", c=128), r=[sbf], w=[wup_scr], owner=sbf)
                    it += 1
        for j in range(NHID):
            st = stage[it % NSTG]
            sbf = stb[it % NSTG]
            dma(st[:, 0:1024], w_down[j * 128:(j + 1) * 128, :], w=[st])
            op("dve",
               lambda e: e.tensor_scalar(out=sbf[:, :], in0=st[:, 0:1024], scalar1=0.5, scalar2=None, op0=ALU.mult),
               r=[st], w=[sbf])
            dma(wdn_scr.ap[j], sbf[:, :], r=[sbf], w=[wdn_scr], owner=sbf)
            it += 1

        cx.barrier()
        es2.close()
        cx.cur_es = es
        X = [sb([128, 2, D], F32, f"X{i}") for i in range(2)]
        junk = sb([128, D], BF16, "junk")
        hn = sb([128, D], BF16, "hn")
        ss = sb([128, 4], F32, "ss")
        hT = sb([128, 8, T], BF16, "hT")
        qT = sb([64, 8, T], BF16, "qT")
        kT = [sb([64, 2, 128], BF16, f"kT{i}") for i in range(3)]
        kTf = sb([64, 2, 128], F32, "kTf")
        Vt = [sb([128, 2, 65], BF16, f"Vt{i}") for i in range(3)]
        Vf = sb([128, 128], F32, "Vf")
        Kf = sb([128, 128], F32, "Kf")
        uT = sb([128, 4, T], BF16, "uT")
        rc = [sb([64, T], F32, "rc0")] * 2
        rs = [sb([64, T], F32, "rs0")] * 2
        PT = sb([128, 4, 2, 128], BF16, "PT")
        attn = sb([128, 512], F32, "attn")
        dn = sb([128, 8], F32, "dn")
        mT = hT
        sc_ab = sb([128, 2, T], F32, "sc_ab")
        sc_a = B(sc_ab.t[:, 0, :], "sc_a"); sc_a.k = sc_ab.k
        sc_b = B(sc_ab.t[:, 1, :], "sc_b"); sc_b.k = sc_ab.k
        SB0 = [sb([128, 2, T], F32, f"SB0{i}") for i in range(2)]
        SB1 = [sb([128, 2, T], F32, f"SB1{i}") for i in range(2)]
        ST0 = [sb([128, 2, T], F32, f"ST0{i}") for i in range(2)]
        ST1 = [sb([128, 2, T], F32, "ST10")] * 2
        ST2 = [sb([128, 2, T], F32, f"ST2{i}") for i in range(2)]
        ST3 = [sb([128, 2, T], F32, "ST30")] * 2
        HBR = [sb([128, 2, T], BF16, f"HBR{i}") for i in range(2)]
        HBI = [sb([128, 2, T], BF16, f"HBI{i}") for i in range(2)]
        G6 = sb([128, 6, 16], F32, "G6")
        GL = sb([128, 2, 16], F32, "GL")
        W1 = [sb([128, T], F32, "W10")] * 2
        W2 = [sb([128, T], F32, "W20")] * 2
        zf = sb([128, 4, T], F32, "zf")
        zb = sb([128, 4, T], BF16, "zb")
        s2 = zf
        sqb = B(junk.t[:].rearrange("p (c t) -> p c t", c=4), "sqb_alias")
        sqb.k = junk.k
        w2 = W2[0]
        rstd_s = sb([128, T], F32, "rstd_s")
        xT2 = hT
        def alias(parent, ap, name):
            b_ = B(ap, name)
            b_.k = parent.k
            return b_

        def flat(bt):
            return bt.t[:].rearrange("p a t -> p (a t)")
        UPS = [[alias(SB0[s_], flat(SB0[s_])[:, 0:T + 2], f"UPa{s_}0"), alias(SB1[s_], flat(SB1[s_])[:, 0:T + 2], f"UPa{s_}1")] for s_ in range(2)]
        CGS = [alias(ST0[s_], ST0[s_].t[:, 0, :], f"cg{s_}") for s_ in range(2)]
        CVS = [alias(ST2[s_], ST2[s_].t[:, 0, :], f"cv{s_}") for s_ in range(2)]
        THS = [alias(ST0[s_], ST0[s_].t[:, 1, :], f"th{s_}") for s_ in range(2)]
        actT = sb([128, NHID, T], BF16, "actT")
        NWU = 3
        NWD = 4
        wup_b = [sb([128, 8, 256], BF16, f"wup{i}") for i in range(NWU)]
        wdn_b = [sb([128, 1024], BF16, f"wdn{i}") for i in range(NWD)]
        Yown = [B(None, "yown0"), B(None, "yown1")]
        Vo = B(attn.t[:, 0:128], "Vo_alias"); Vo.k = attn.k
        Ko = B(attn.t[:, 128:256], "Ko_alias"); Ko.k = attn.k
        Ho = sb([128, 2, 16], F32, "Ho"); To = sb([128, 44, 2], F32, "To")
        ring = [0]
        op("pool", lambda e: e.memset(kTf[:], 0.0), w=[kTf])
        for i_ in range(3):
            op("pool", lambda e: e.memset(kT[i_][:], 0.0), w=[kT[i_]])
            op("pool", lambda e: e.memset(Vt[i_][:], 0.0), w=[Vt[i_]])

        def rms_rstd(dst, src_ss, n, eps):
            np_ = dst.shape[0]
            op("act", lambda e: e.activation(out=dst, in_=src_ss, func=AF.Ln, scale=1.0 / n, bias=eps_c[0:np_, :]), r=[ss, eps_c], w=[ss])
            op("act", lambda e: e.activation(out=dst, in_=dst, func=AF.Exp, scale=-0.5), r=[ss], w=[ss])

        def norm_transpose(Xb, nt, ts, dstT):
            for i in range(nt):
                op("act", lambda e: e.activation(out=junk[:ts, :], in_=Xb[:ts, i, :], func=AF.Square, accum_out=ss[:ts, i:i + 1]),
                   r=[Xb], w=[junk, ss])
                rms_rstd(ss[:ts, i:i + 1], ss[:ts, i:i + 1], D, EPS)
                op("dve", lambda e: e.tensor_scalar(out=hn[:ts, :], in0=Xb[:ts, i, :], scalar1=ss[:ts, i:i + 1], scalar2=None, op0=ALU.mult),
                   r=[Xb, ss], w=[hn])
                pb = nb()
                pv = pb[:].bitcast(BF16)
                for kt in range(8):
                    op("pe", lambda e: e.transpose(out=pv[:, kt * 128:kt * 128 + ts], in_=hn[:ts, kt * 128:(kt + 1) * 128], identity=ident_b[:ts, :ts]),
                       r=[hn, ident_b], w=[pb], inc=(kt == 7))
                op("act", lambda e: e.copy(out=dstT[:, :, i * 128:i * 128 + ts],
                                           in_=pv.rearrange("p (k t) -> p k t", k=8)[:, :, 0:ts]), r=[pb], w=[dstT])

        wcnt = dict(ui=0, uu=0, di=0, du=0)

        def up_issue(j):
            wb = wup_b[wcnt["ui"] % NWU]
            wcnt["ui"] += 1
            dma(wb[:].rearrange("p k c -> p (k c)"), wup_scr.ap[j], r=[wup_scr], w=[wb])

        def up_get():
            wb = wup_b[wcnt["uu"] % NWU]
            wcnt["uu"] += 1
            return wb

        def dn_issue(j):
            wd = wdn_b[wcnt["di"] % NWD]
            wcnt["di"] += 1
            dma(wd[:], wdn_scr.ap[j], r=[wdn_scr], w=[wd])

        def dn_get():
            wd = wdn_b[wcnt["du"] % NWD]
            wcnt["du"] += 1
            return wd

        cur_spec = [None, None]

        def issue_x(sp_, bi_):
            if sp_ is None or sp_.get("x_loaded"):
                return
            sp_["x_loaded"] = True
            Xn = X[bi_ % 2]
            if sp_.get("nreal") is None:
                ts_ = min(sp_["Tn"], 128)
                nt_ = (sp_["Tn"] + 127) // 128
                dma(Xn[:ts_, 0:nt_, :], sp_["xsrc"].rearrange("(n p) d -> p n d", p=ts_), w=[Xn])
            else:
                dma(Xn[:sp_["nreal"], 0, :], sp_["xsrc"], w=[Xn])

        def issue_rope(sp_, bi_):
            if sp_ is None or sp_.get("rope_loaded") or sp_["mode"] == "pre":
                return
            sp_["rope_loaded"] = True
            ro = sp_.get("rope_off", 0)
            dma(rc[bi_ % 2][:, 0:sp_["Tn"]], ropec_in[:, ro:ro + sp_["Tn"]], w=[rc[bi_ % 2]])
            dma(rs[bi_ % 2][:, 0:sp_["Tn"]], ropes_in[:, ro:ro + sp_["Tn"]], w=[rs[bi_ % 2]])

        def block(bi, me, nx, xsrc, Tn, mode, rope_off=0, first_main=False, ydst=None, samp=None, last=False, nreal=None):
            ts = min(Tn, 128)
            nt = (Tn + 127) // 128
            Xb = X[bi % 2]
            nr = nreal if nreal is not None else ts
            issue_x(me, bi)
            issue_rope(me, bi)
            sl_win = ring[0]
            if mode != "pre":
                rcb, rsb = rc[bi % 2], rs[bi % 2]
            if mode != "pre":
                for j_ in range(NWU):
                    up_issue(j_)
            norm_transpose(Xb, nt, ts, hT)
            def proj(col0, m, evac):
                pb = nb()
                for kt in range(8):
                    op("pe", lambda e: e.matmul(pb[0:m, 0:Tn], lhsT=Win[:, kt, col0:col0 + m], rhs=hT[:, kt, 0:Tn], start=(kt == 0), stop=(kt == 7)),
                       r=[Win, hT], w=[pb], inc=(kt == 7))
                return pb
            for c in range(4):
                pb = proj(1280 + c * 128, 128, None)
                op("act", lambda e: e.copy(out=uT[:, c, 0:Tn], in_=pb[:, 0:Tn]), r=[pb], w=[uT])
            if mode != "pre":
                for h in range(8):
                    pa = proj(h * 64, 64, None)
                    pr_ = proj(512 + h * 64, 64, None)
                    op("dve", lambda e: e.tensor_tensor(out=sc_a[0:64, 0:Tn], in0=pa[0:64, 0:Tn], in1=rcb[:, 0:Tn], op=ALU.mult), r=[pa, rcb], w=[sc_a])
                    op("dve", lambda e: e.tensor_tensor(out=sc_b[0:64, 0:Tn], in0=pr_[0:64, 0:Tn], in1=rsb[:, 0:Tn], op=ALU.mult), r=[pr_, rsb], w=[sc_b])
                    op("pool", lambda e: e.tensor_tensor(out=qT[:, h, 0:Tn], in0=sc_a[0:64, 0:Tn], in1=sc_b[0:64, 0:Tn], op=ALU.add), r=[sc_a, sc_b], w=[qT])
                slots = [(ring[0] + 1 + i) % 3 for i in range(nt)]
                for g in range(2):
                    pa = proj(1024 + g * 64, 64, None)
                    pr_ = proj(1152 + g * 64, 64, None)
                    op("dve", lambda e: e.tensor_tensor(out=sc_a[0:64, 0:Tn], in0=pa[0:64, 0:Tn], in1=rcb[:, 0:Tn], op=ALU.mult), r=[pa, rcb], w=[sc_a])
                    op("dve", lambda e: e.tensor_tensor(out=sc_b[0:64, 0:Tn], in0=pr_[0:64, 0:Tn], in1=rsb[:, 0:Tn], op=ALU.mult), r=[pr_, rsb], w=[sc_b])
                    for i in range(nt):
                        op("pool", lambda e: e.tensor_tensor(out=kT[slots[i]][:, g, 0:ts], in0=sc_a[0:64, i * 128:i * 128 + ts],
                                                             in1=sc_b[0:64, i * 128:i * 128 + ts], op=ALU.add), r=[sc_a, sc_b], w=[kT[slots[i]]])
                    if last or samp is not None:
                        i = nt - 1
                        op("pool", lambda e: e.tensor_tensor(out=kTf[:, g, 0:ts], in0=sc_a[0:64, i * 128:i * 128 + ts],
                                                             in1=sc_b[0:64, i * 128:i * 128 + ts], op=ALU.add), r=[sc_a, sc_b], w=[kTf])
                for i in range(nt):
                    pb = nb()
                    for kt in range(8):
                        op("pe", lambda e: e.matmul(pb[0:ts, 0:128], lhsT=hT[:, kt, i * 128:i * 128 + ts], rhs=Win[:, kt, 1792:1920], start=(kt == 0), stop=(kt == 7)),
                           r=[Win, hT], w=[pb], inc=(kt == 7))
                    vs_ = Vt[slots[i]]
                    op("pool", lambda e: e.memset(vs_[:, :, 64:65], 1.0), w=[vs_])
                    op("act", lambda e: e.copy(out=vs_[0:ts, :, 0:64], in_=pb[0:ts, 0:128].rearrange("p (g d) -> p g d", g=2)), r=[pb, vs_], w=[vs_])
                    if (last or samp is not None) and i == nt - 1:
                        op("dve", lambda e: e.tensor_copy(out=Vo[0:ts, :], in_=pb[0:ts, 0:128]), r=[pb], w=[Vo])
                        for g in range(2):
                            pk = nb()
                            op("pe", lambda e: e.transpose(out=pk[0:128, 0:64], in_=kTf[:, g, 0:128], identity=ident_f[0:64, 0:64]), r=[kTf, ident_f], w=[pk])
                            op("act", lambda e: e.copy(out=Ko[0:ts, g * 64:(g + 1) * 64], in_=pk[0:ts, 0:64]), r=[pk], w=[Ko])
                        if samp is not None:
                            dma(ks_out[samp, 112:128, :], Ko[0:16, :], r=[Ko], owner=Ko)
                            dma(vs_out[samp, 112:128, :], Vo[0:16, :], r=[Vo], owner=Vo)
                        else:
                            dma(k_out, Ko[:, :], r=[Ko], owner=Ko)
                            dma(v_out, Vo[:, :], r=[Vo], owner=Vo)
                        out_owners.extend([Ko, Vo])
                yield
                if samp is not None:
                    dma(Hc[:, 0, :], st_re[samp].rearrange("(q gl) p -> (gl p) q", gl=2), w=[Hc], slow=True)
                    dma(Hc[:, 1, :], st_im[samp].rearrange("(q gl) p -> (gl p) q", gl=2), w=[Hc], slow=True)
                    for r_ in range(2):
                        dma(TAIL[:, :, r_], st_conv[samp, r_].rearrange("(c p) -> p c", p=128), w=[TAIL], slow=True)
                    sl = sl_win
                    dma(Kf[:], cache_k[samp], w=[Kf])
                    dma(Vf[:], cache_v[samp], w=[Vf])
                    for g in range(2):
                        pb = nb()
                        op("pe", lambda e: e.transpose(out=pb[0:64, 0:128], in_=Kf[:, g * 64:(g + 1) * 64], identity=ident_f[:]), r=[Kf, ident_f], w=[pb])
                        op("act", lambda e: e.copy(out=kT[sl][:, g, :], in_=pb[0:64, 0:128]), r=[pb], w=[kT[sl]])
                    op("pool", lambda e: e.memset(Vt[sl][:, :, 64:65], 1.0), w=[Vt[sl]])
                    op("dve", lambda e: e.tensor_copy(out=Vt[sl][:, :, 0:64], in_=Vf[:].rearrange("p (g d) -> p g d", g=2)), r=[Vf, Vt[sl]], w=[Vt[sl]])
                    dma(ks_out[samp, 0:112, :], Kf[16:128, :], r=[Kf], owner=Kf)
                    dma(vs_out[samp, 0:112, :], Vf[16:128, :], r=[Vf], owner=Vf)
                    out_owners.extend([Kf, Vf])
                def attn_unit(i, g):
                    sl_prev = (slots[i] + 2) % 3
                    sl_cur = slots[i]
                    p0 = nb(); p1 = nb()
                    pss = (p0, p1)
                    for hh in range(4):
                        h = 4 * g + hh
                        for kt_, slk, nk in ((0, sl_prev, 128), (1, sl_cur, ts)):
                            op("pe", lambda e: e.matmul(pss[kt_][0:nk, hh * 128:hh * 128 + ts], lhsT=kT[slk][:, g, 0:nk], rhs=qT[:, h, i * 128:i * 128 + ts],
                                                        start=True, stop=True), r=[kT[slk], qT], w=[pss[kt_]], inc=(hh == 3))
                    nqh = 2 if ts == 128 else 1
                    for kt_, nk in ((0, 128), (1, ts)):
                        for qh in range(nqh):
                            qw = 64 if nqh == 2 else ts
                            bias = None
                            if nqh == 2:
                                if kt_ == 0:
                                    if first_main and i == 0:
                                        bias = maskc[:, qh:qh + 1]
                                    elif qh == 1:
                                        bias = maskc[:, 2:3]
                                elif qh == 0:
                                    bias = maskc[:, 3:4] if nreal is None else maskc[:, 4:5]
                            src = pss[kt_][0:nk, :].rearrange("p (h q) -> p h q", h=4)[:, :, qh * qw:(qh + 1) * qw]
                            dst = PT[0:nk, :, kt_, qh * qw:(qh + 1) * qw]
                            if bias is None:
                                op("act", lambda e: e.activation(out=dst, in_=src, func=AF.Exp, scale=0.125), r=[pss[kt_]], w=[PT])
                            else:
                                op("act", lambda e: e.activation(out=dst, in_=src, func=AF.Exp, scale=0.125, bias=bias[0:nk, :]), r=[pss[kt_], maskc], w=[PT])
                    po = nb()
                    for hh in range(4):
                        for kt_, slk, nk in ((0, sl_prev, 128), (1, sl_cur, ts)):
                            op("pe", lambda e: e.matmul(po[0:ts, hh * 65:hh * 65 + 65], lhsT=PT[0:nk, hh, kt_, 0:ts], rhs=Vt[slk][0:nk, g, :],
                                                        start=(kt_ == 0), stop=(kt_ == 1)), r=[PT, Vt[slk]], w=[po], inc=(hh == 3 and kt_ == 1))
                    return po

                def attn_back(i, g, po):
                    pov = po[0:ts, 0:260].rearrange("p (h d) -> p h d", h=4)
                    op("dve", lambda e: e.tensor_tensor(out=dn[0:ts, 4 * g:4 * g + 4], in0=pov[:, :, 64], in1=ES[0:ts, 4 * g:4 * g + 4], op=ALU.add), r=[po, ES], w=[dn])
                    op("dve", lambda e: e.reciprocal(out=dn[0:ts, 4 * g:4 * g + 4], in_=dn[0:ts, 4 * g:4 * g + 4]), r=[dn], w=[dn])
                    op("dve", lambda e: e.tensor_tensor(out=attn[0:ts, 256 * g:256 * g + 256].rearrange("p (h d) -> p h d", h=4), in0=pov[:, :, 0:64],
                                                        in1=dn[0:ts, 4 * g:4 * g + 4][:, :, None].broadcast_to([ts, 4, 64]), op=ALU.mult), r=[po, dn], w=[attn])

                def attn_norm(i):
                    op("act", lambda e: e.activation(out=junk[:ts, 0:512], in_=attn[:ts, :], func=AF.Square, accum_out=ss[:ts, 2:3]), r=[attn], w=[junk, ss])
                    rms_rstd(ss[:ts, 2:3], ss[:ts, 2:3], 512, EPS)
                    op("dve", lambda e: e.tensor_scalar(out=hn[:ts, 0:512], in0=attn[:ts, :], scalar1=ss[:ts, 2:3], scalar2=None, op0=ALU.mult), r=[attn, ss], w=[hn])
                    pb = nb()
                    pv = pb[:].bitcast(BF16)
                    for kt in range(4):
                        op("pe", lambda e: e.transpose(out=pv[:, kt * 128:kt * 128 + ts], in_=hn[:ts, kt * 128:(kt + 1) * 128], identity=ident_b[:ts, :ts]),
                           r=[hn, ident_b], w=[pb], inc=(kt == 3))
                    op("act", lambda e: e.copy(out=mT[:, 0:4, i * 128:i * 128 + ts], in_=pv[:, 0:512].rearrange("p (k t) -> p k t", k=4)[:, :, 0:ts]), r=[pb], w=[mT])
                ring[0] = slots[-1]
            else:
                yield
            nseg = (Tn + SEG - 1) // SEG
            sl = min(SEG, Tn)
            lastc = (sl if nreal is None else nreal) - 1
            py = None
            assert nseg == 1
            c1a = Ec[:, :, 1]; s1a = Es[:, :, 1]
            op("dve", lambda e: e.tensor_tensor(out=G6[:, 0, :], in0=Hc[:, 0, :], in1=c1a, op=ALU.mult), r=[Hc, Ec], w=[G6])
            op("dve", lambda e: e.tensor_tensor(out=G6[:, 1, :], in0=Hc[:, 1, :], in1=s1a, op=ALU.mult), r=[Hc, Es, G6], w=[G6])
            op("dve", lambda e: e.tensor_tensor(out=G6[:, 2, :], in0=Hc[:, 0, :], in1=s1a, op=ALU.mult), r=[Hc, Es, G6], w=[G6])
            op("dve", lambda e: e.tensor_tensor(out=G6[:, 3, :], in0=Hc[:, 1, :], in1=c1a, op=ALU.mult), r=[Hc, Ec, G6], w=[G6])
            op("dve", lambda e: e.tensor_tensor(out=G6[:, 4, :], in0=G6[:, 0, :], in1=G6[:, 1, :], op=ALU.subtract), r=[G6], w=[G6])
            op("dve", lambda e: e.tensor_tensor(out=G6[:, 5, :], in0=G6[:, 2, :], in1=G6[:, 3, :], op=ALU.add), r=[G6], w=[G6])
            pyb = {}

            def ssm_front(hg):
                par = hg % 2
                c = hg // 2
                q0 = 2 * hg
                b0, b1, t0, t1, t2, t3 = SB0[par], SB1[par], ST0[par], ST1[par], ST2[par], ST3[par]
                pre_ = nb(); pim = nb()
                for pl in range(2):
                    q = q0 + pl
                    op("pe", lambda e: e.matmul(pre_[:, pl * Tn:(pl + 1) * Tn], lhsT=Bz[:, q, 0, :], rhs=uT[:, c, 0:Tn], start=True, stop=True), r=[Bz, uT], w=[pre_], inc=False)
                    op("pe", lambda e: e.matmul(pim[:, pl * Tn:(pl + 1) * Tn], lhsT=Bz[:, q, 1, :], rhs=uT[:, c, 0:Tn], start=True, stop=True), r=[Bz, uT], w=[pim], inc=(pl == 1))
                op("act", lambda e: e.copy(out=b0[:, :, 0:Tn], in_=pre_[:, 0:2 * Tn].rearrange("p (a t) -> p a t", a=2)), r=[pre_], w=[b0])
                op("act", lambda e: e.copy(out=b1[:, :, 0:Tn], in_=pim[:, 0:2 * Tn].rearrange("p (a t) -> p a t", a=2)), r=[pim], w=[b1])

                def v4(bt):
                    return bt[:, :, 0:Tn].rearrange("p a (s l) -> p a s l", s=nseg)
                ecb = Ec[:, q0:q0 + 2, 0:sl][:, :, None, :].broadcast_to([128, 2, nseg, sl])
                esb = Es[:, q0:q0 + 2, 0:sl][:, :, None, :].broadcast_to([128, 2, nseg, sl])
                op("dve", lambda e: e.tensor_tensor(out=v4(t0), in0=v4(b0), in1=ecb, op=ALU.mult), r=[b0, Ec], w=[t0])
                op("dve", lambda e: e.tensor_tensor(out=v4(t1), in0=v4(b1), in1=esb, op=ALU.mult), r=[b1, Es], w=[t1])
                op("dve", lambda e: e.tensor_tensor(out=v4(t2), in0=v4(b1), in1=ecb, op=ALU.mult), r=[b1, Ec], w=[t2])
                op("dve", lambda e: e.tensor_tensor(out=v4(t3), in0=v4(b0), in1=esb, op=ALU.mult), r=[b0, Es], w=[t3])
                op("dve", lambda e: e.tensor_tensor(out=v4(t0), in0=v4(t0), in1=v4(t1), op=ALU.add), r=[t0, t1], w=[t0])
                op("dve", lambda e: e.tensor_tensor(out=v4(t2), in0=v4(t2), in1=v4(t3), op=ALU.subtract), r=[t2, t3], w=[t2])
                for pl in range(2):
                    q = q0 + pl
                    rb = rmag[:, q:q + 1].broadcast_to([128, sl])
                    sre = t0[:, pl, 0:sl]
                    sim_ = t2[:, pl, 0:sl]
                    op("dve", lambda e: e.tensor_tensor_scan(out=sre, data0=rb, data1=sre, initial=G6[:, 4, q:q + 1], op0=ALU.mult, op1=ALU.add), r=[t0, rmag, G6], w=[t0])
                    op("dve", lambda e: e.tensor_tensor_scan(out=sim_, data0=rb, data1=sim_, initial=G6[:, 5, q:q + 1], op0=ALU.mult, op1=ALU.add), r=[t2, rmag, G6], w=[t2])
                op("act", lambda e: e.copy(out=GL[:, 0, q0:q0 + 2], in_=t0[:, :, lastc]), r=[t0, GL], w=[GL])
                op("act", lambda e: e.copy(out=GL[:, 1, q0:q0 + 2], in_=t2[:, :, lastc]), r=[t2, GL], w=[GL])
                if mode == "pre":
                    return
                op("dve", lambda e: e.tensor_tensor(out=v4(t1), in0=v4(t0), in1=ecb, op=ALU.mult), r=[t0, Ec], w=[t1])
                op("dve", lambda e: e.tensor_tensor(out=v4(t3), in0=v4(t2), in1=esb, op=ALU.mult), r=[t2, Es], w=[t3])
                op("dve", lambda e: e.tensor_tensor(out=v4(b0), in0=v4(t1), in1=v4(t3), op=ALU.subtract), r=[t1, t3], w=[b0])
                op("dve", lambda e: e.tensor_tensor(out=v4(b1), in0=v4(t2), in1=ecb, op=ALU.mult), r=[t2, Ec], w=[b1])
                op("dve", lambda e: e.tensor_tensor(out=v4(sc_ab), in0=v4(t0), in1=esb, op=ALU.mult), r=[t0, Es], w=[sc_ab])
                op("dve", lambda e: e.tensor_tensor(out=v4(b1), in0=v4(b1), in1=v4(sc_ab), op=ALU.add), r=[b1, sc_ab], w=[b1])
                hbr, hbi = HBR[par], HBI[par]
                op("act", lambda e: e.copy(out=hbr[:, :, 0:Tn], in_=b0[:, :, 0:Tn]), r=[b0], w=[hbr])
                op("act", lambda e: e.copy(out=hbi[:, :, 0:Tn], in_=b1[:, :, 0:Tn]), r=[b1], w=[hbi])

            def ssm_back(hg):
                par = hg % 2
                c = hg // 2
                q0 = 2 * hg
                hbr, hbi = HBR[par], HBI[par]
                py = pyb.get(c)
                for pl in range(2):
                    q = q0 + pl
                    qm = q % 4
                    if qm == 0:
                        py = nb()
                        pyb[c] = py
                    op("pe", lambda e: e.matmul(py[32 * qm:32 * qm + 32, 0:Tn], lhsT=Cz[:, q, 0, :], rhs=hbr[:, pl, 0:Tn], start=True, stop=False,
                                                tile_position=(0, 32 * qm)), r=[Cz, hbr], w=[py], inc=False)
                    op("pe", lambda e: e.matmul(py[32 * qm:32 * qm + 32, 0:Tn], lhsT=Cz[:, q, 1, :], rhs=hbi[:, pl, 0:Tn], start=False, stop=True,
                                                tile_position=(0, 32 * qm)), r=[Cz, hbi], w=[py])
                if hg % 2 == 1:
                    w1 = W1[c % 2]; w2 = W2[c % 2]
                    yv = w1
                    op("dve", lambda e: e.scalar_tensor_tensor(out=yv[:, 0:Tn], in0=uT[:, c, 0:Tn], scalar=Dcol[:, c:c + 1], in1=py[:, 0:Tn], op0=ALU.mult, op1=ALU.add),
                       r=[uT, Dcol, py], w=[w1])
                    op("act", lambda e: e.activation(out=w2[:, 0:Tn], in_=yv[:, 0:Tn], func=AF.Square), r=[w1], w=[w2])
                    op("dve", lambda e: e.tensor_scalar(out=w2[:, 0:Tn], in0=w2[:, 0:Tn], scalar1=0.044715, scalar2=1.0, op0=ALU.mult, op1=ALU.add), r=[w2], w=[w2])
                    op("dve", lambda e: e.tensor_tensor(out=w2[:, 0:Tn], in0=w2[:, 0:Tn], in1=yv[:, 0:Tn], op=ALU.mult), r=[w2, w1], w=[w2])
                    op("act", lambda e: e.activation(out=w2[:, 0:Tn], in_=w2[:, 0:Tn], func=AF.Tanh, scale=0.7978845608028654), r=[w2], w=[w2])
                    op("dve", lambda e: e.tensor_scalar(out=w2[:, 0:Tn], in0=w2[:, 0:Tn], scalar1=1.0, scalar2=0.5, op0=ALU.add, op1=ALU.mult), r=[w2], w=[w2])
                    op("dve", lambda e: e.tensor_tensor(out=zf[:, c, 0:Tn], in0=w2[:, 0:Tn], in1=yv[:, 0:Tn], op=ALU.mult), r=[w2, w1], w=[zf])
                    op("pool", lambda e: e.tensor_copy(out=zb[:, c, 0:Tn], in_=zf[:, c, 0:Tn]), r=[zf], w=[zb])

            units = [(i_, g_) for i_ in range(nt) for g_ in range(2)] if mode != "pre" else []
            ui = 0
            pend = None
            for hg in range(8):
                if pend is not None:
                    attn_back(*pend)
                    if pend[1] == 1:
                        attn_norm(pend[0])
                    pend = None
                if mode != "pre" and hg % 2 == 1 and ui < len(units):
                    pend = (units[ui][0], units[ui][1], attn_unit(*units[ui]))
                    ui += 1
                ssm_front(hg)
                if mode == "pre":
                    continue
                if hg >= 1:
                    ssm_back(hg - 1)
            if mode != "pre":
                if pend is not None:
                    attn_back(*pend)
                    if pend[1] == 1:
                        attn_norm(pend[0])
                    pend = None
                ssm_back(7)
                while ui < len(units):
                    po_ = attn_unit(*units[ui])
                    attn_back(units[ui][0], units[ui][1], po_)
                    if units[ui][1] == 1:
                        attn_norm(units[ui][0])
                    ui += 1
            w2 = W2[0]
            cLa = Ec[:, :, lastc]; sLa = Es[:, :, lastc]
            op("dve", lambda e: e.tensor_tensor(out=G6[:, 0, :], in0=GL[:, 0, :], in1=cLa, op=ALU.mult), r=[GL, Ec, G6], w=[G6])
            op("dve", lambda e: e.tensor_tensor(out=G6[:, 1, :], in0=GL[:, 1, :], in1=sLa, op=ALU.mult), r=[GL, Es, G6], w=[G6])
            op("dve", lambda e: e.tensor_tensor(out=G6[:, 2, :], in0=GL[:, 1, :], in1=cLa, op=ALU.mult), r=[GL, Ec, G6], w=[G6])
            op("dve", lambda e: e.tensor_tensor(out=G6[:, 3, :], in0=GL[:, 0, :], in1=sLa, op=ALU.mult), r=[GL, Es, G6], w=[G6])
            op("dve", lambda e: e.tensor_tensor(out=Hc[:, 0, :], in0=G6[:, 0, :], in1=G6[:, 1, :], op=ALU.subtract), r=[G6, Hc], w=[Hc])
            op("dve", lambda e: e.tensor_tensor(out=Hc[:, 1, :], in0=G6[:, 2, :], in1=G6[:, 3, :], op=ALU.add), r=[G6, Hc], w=[Hc])
            if mode == "pre":
                return
            for c2 in range(4):
                pg = nb()
                for c in range(4):
                    op("pe", lambda e: e.matmul(pg[:, 0:Tn], lhsT=Wglu[:, c, c2 * 128:(c2 + 1) * 128], rhs=zb[:, c, 0:Tn], start=(c == 0), stop=(c == 3)),
                       r=[Wglu, zb], w=[pg], inc=(c == 3))
                op("act", lambda e: e.activation(out=w2[:, 0:Tn], in_=pg[:, 0:Tn], func=AF.Tanh, scale=0.5), r=[pg], w=[w2])
                op("dve", lambda e: e.scalar_tensor_tensor(out=s2[:, c2, 0:Tn], in0=w2[:, 0:Tn], scalar=1.0, in1=zf[:, c2, 0:Tn], op0=ALU.add, op1=ALU.mult),
                   r=[w2, zf], w=[s2])
                op("act", lambda e: e.activation(out=sqb[:, c2, 0:Tn], in_=s2[:, c2, 0:Tn], func=AF.Square), r=[s2], w=[sqb])
            pn = nb()
            for c in range(4):
                op("pe", lambda e: e.matmul(pn[:, 0:Tn], lhsT=ones_b[:], rhs=sqb[:, c, 0:Tn], start=(c == 0), stop=(c == 3)), r=[ones_b, sqb], w=[pn], inc=(c == 3))
            op("act", lambda e: e.activation(out=rstd_s[:, 0:Tn], in_=pn[:, 0:Tn], func=AF.Ln, scale=1.0 / 512, bias=eps4_c[:, :]), r=[pn, eps4_c], w=[rstd_s])
            op("act", lambda e: e.activation(out=rstd_s[:, 0:Tn], in_=rstd_s[:, 0:Tn], func=AF.Exp, scale=-0.5), r=[rstd_s], w=[rstd_s])
            for c in range(4):
                op("pool", lambda e: e.tensor_tensor(out=mT[:, 4 + c, 0:Tn], in0=s2[:, c, 0:Tn], in1=rstd_s[:, 0:Tn], op=ALU.mult), r=[s2, rstd_s], w=[mT])
            for i in range(nt):
                for hf in range(2):
                    pb = nb()
                    for kt in range(8):
                        op("pe", lambda e: e.matmul(pb[0:ts, :], lhsT=mT[:, kt, i * 128:i * 128 + ts], rhs=Wout[:, kt, hf * 512:(hf + 1) * 512], start=(kt == 0), stop=(kt == 7)),
                           r=[mT, Wout], w=[pb], inc=(kt == 7))
                    op("dve", lambda e: e.tensor_tensor(out=Xb[0:ts, i, hf * 512:(hf + 1) * 512], in0=Xb[0:ts, i, hf * 512:(hf + 1) * 512], in1=pb[0:ts, :], op=ALU.add),
                       r=[Xb, pb], w=[Xb])
            norm_transpose(Xb, nt, ts, xT2)
            halo = (mode == "halo")
            if not halo:
                for j_ in range(NWD):
                    dn_issue(j_)
            issue_rope(nx, bi + 1)
            tcol = Tn if nreal is None else nreal
            for j in range(NHID):
                wb = up_get()
                pgv = []
                for gv in range(2):
                    pb = nb()
                    for kt in range(8):
                        op("pe", lambda e: e.matmul(pb[:, 0:Tn], lhsT=wb[:, kt, gv * 128:(gv + 1) * 128], rhs=xT2[:, kt, 0:Tn], start=(kt == 0), stop=(kt == 7)),
                           r=[wb, xT2], w=[pb], inc=(kt == 7))
                    pgv.append(pb)
                if j + NWU < NHID:
                    up_issue(j + NWU)
                cg, cv, th = CGS[j % 2], CVS[j % 2], THS[j % 2]
                outs = (cg, cv)
                for gv in range(2):
                    hc = gv * NHID + j
                    u_ = UPS[j % 2][gv]
                    op("pool", lambda e: e.tensor_copy(out=u_[:, 0:2], in_=TAIL[:, hc, :]), r=[TAIL], w=[u_])
                    op("act", lambda e: e.copy(out=u_[:, 2:2 + Tn], in_=pgv[gv][:, 0:Tn]), r=[pgv[gv], u_], w=[u_])
                    op("pool", lambda e: e.tensor_copy(out=TAIL[:, hc, :], in_=u_[:, tcol:tcol + 2]), r=[u_, TAIL], w=[TAIL])
                    if halo:
                        continue
                    o_ = outs[gv]
                    eng = "dve"
                    op("act", lambda e: e.activation(out=o_[:, 0:Tn], in_=u_[:, 2:2 + Tn], func=AF.Identity, scale=cw[:, 2, hc:hc + 1], bias=cb[:, hc:hc + 1]),
                       r=[u_, cw, cb], w=[o_])
                    op(eng, lambda e: e.scalar_tensor_tensor(out=o_[:, 0:Tn], in0=u_[:, 1:1 + Tn], scalar=cw[:, 1, hc:hc + 1], in1=o_[:, 0:Tn], op0=ALU.mult, op1=ALU.add),
                       r=[u_, cw, o_], w=[o_])
                    op(eng, lambda e: e.scalar_tensor_tensor(out=o_[:, 0:Tn], in0=u_[:, 0:Tn], scalar=cw[:, 0, hc:hc + 1], in1=o_[:, 0:Tn], op0=ALU.mult, op1=ALU.add),
                       r=[u_, cw, o_], w=[o_])
                if halo:
                    continue
                op("act", lambda e: e.activation(out=th[:, 0:Tn], in_=cg[:, 0:Tn], func=AF.Tanh, scale=0.5), r=[cg], w=[th])
                op("dve", lambda e: e.tensor_tensor(out=cv[:, 0:Tn], in0=cg[:, 0:Tn], in1=cv[:, 0:Tn], op=ALU.mult), r=[cg, cv], w=[cv])
                op("dve", lambda e: e.scalar_tensor_tensor(out=actT[:, j, 0:Tn], in0=th[:, 0:Tn], scalar=1.0, in1=cv[:, 0:Tn], op0=ALU.add, op1=ALU.mult),
                   r=[th, cv], w=[actT])
            if halo:
                return
            yield
            accs = [nb() for _ in range(nt * 2)]
            for j in range(NHID):
                wd = dn_get()
                for i in range(nt):
                    for hf in range(2):
                        pb = accs[i * 2 + hf]
                        op("pe", lambda e: e.matmul(pb[0:ts, :], lhsT=actT[:, j, i * 128:i * 128 + ts], rhs=wd[:, hf * 512:(hf + 1) * 512], start=(j == 0), stop=(j == NHID - 1)),
                           r=[actT, wd], w=[pb], inc=(j == NHID - 1 or (i == nt - 1 and hf == 1)))
                if j + NWD < NHID:
                    dn_issue(j + NWD)
            for i in range(nt):
                for hf in range(2):
                    pb = accs[i * 2 + hf]
                    op("dve", lambda e: e.tensor_tensor(out=Xb[0:ts, i, hf * 512:(hf + 1) * 512], in0=Xb[0:ts, i, hf * 512:(hf + 1) * 512], in1=pb[0:ts, :], op=ALU.add),
                       r=[Xb, pb], w=[Xb])
                op("act", lambda e: e.activation(out=junk[:ts, :], in_=Xb[:ts, i, :], func=AF.Square, accum_out=ss[:ts, 3:4]), r=[Xb], w=[junk, ss])
                rms_rstd(ss[:ts, 3:4], ss[:ts, 3:4], D, EPS)
                op("dve", lambda e: e.scalar_tensor_tensor(out=Xb[0:ts, i, :], in0=Xb[0:ts, i, :], scalar=ss[:ts, 3:4], in1=GF[0:ts, :], op0=ALU.mult, op1=ALU.mult),
                   r=[Xb, ss, GF], w=[Xb])
            if nreal is None:
                dma(ydst.rearrange("(n p) d -> p n d", p=ts), Xb[0:ts, 0:nt, :], r=[Xb], owner=Yown[bi % 2])
            else:
                dma(ydst, Xb[0:nr, 0, :], r=[Xb], owner=Yown[bi % 2])
            out_owners.append(Yown[bi % 2])
            if os.environ.get("KDBG") and ydst is not None and samp is None:
                dma(ydst[0:128, 0:256], zf[:, 0, :], r=[zf], owner=Yown[bi % 2])
                dma(ydst[128:256, 0:512], attn[:, :], r=[attn], owner=Yown[bi % 2])
                dbg_t = sb([128, 256], F32, "dbg_t")
                op("dve", lambda e: e.tensor_copy(out=dbg_t[:], in_=mT[:, 5, :]), r=[mT], w=[dbg_t])
                dma(ydst[0:128, 256:512], dbg_t[:], r=[dbg_t], owner=Yown[bi % 2])
                dbg_u = sb([128, 256], F32, "dbg_u")
                op("dve", lambda e: e.tensor_copy(out=dbg_u[:], in_=HBR[1][:, 1, :]), r=[HBR[1]], w=[dbg_u])
                dma(ydst[0:128, 512:768], dbg_u[:], r=[dbg_u], owner=Yown[bi % 2])
            if samp is not None or last:
                op("act", lambda e: e.copy(out=Ho[:], in_=Hc[:]), r=[Hc], w=[Ho])
                op("act", lambda e: e.copy(out=To[:], in_=TAIL[:]), r=[TAIL], w=[To])
                d_re = sres_out[samp] if samp is not None else sre_out
                d_im = sims_out[samp] if samp is not None else sim_out
                d_cv = convs_out[samp] if samp is not None else conv_out
                dma(d_re.rearrange("(q gl) p -> (gl p) q", gl=2), Ho[:, 0, :], r=[Ho], owner=Ho, slow=True)
                dma(d_im.rearrange("(q gl) p -> (gl p) q", gl=2), Ho[:, 1, :], r=[Ho], owner=Ho, slow=True)
                for r_ in range(2):
                    dma(d_cv[r_].rearrange("(c p) -> p c", p=128), To[:, :, r_], r=[To], owner=To, slow=True)
                out_owners.extend([Ho, To])

        specs = []
        for b in range(n_pre if KSTOP >= 2 else 0):
            specs.append(dict(xsrc=xprev[b * T:(b + 1) * T, :], Tn=T, mode="pre"))
        if KSTOP >= 3:
            specs.append(dict(xsrc=xprev[NP:NP + T, :], Tn=T, mode="halo", rope_off=0))
        for b in range(n_main if KSTOP >= 4 else 0):
            specs.append(dict(xsrc=xmain[b * T:(b + 1) * T, :], Tn=T, mode="full", rope_off=T + b * T, first_main=(b == 0),
                              ydst=y_main[b * T:(b + 1) * T, :], last=(b == n_main - 1)))
        if do_sample and KSTOP >= 5:
            for s_ in range(2):
                specs.append(dict(xsrc=xs[s_], Tn=128, mode="full", rope_off=T + NM + 128 * s_, ydst=y_s[s_], samp=s_, nreal=16))
        nsp = len(specs)
        gens = []
        for si, sp_ in enumerate(specs):
            kw = {k_: v_ for k_, v_ in sp_.items() if k_ not in ("x_loaded", "rope_loaded")}
            gens.append(block(si, sp_, specs[si + 1] if si + 1 < nsp else None, **kw))
        for si in range(min(2, nsp)):
            issue_x(specs[si], si)
        if nsp:
            next(gens[0], None)
        for si in range(nsp):
            next(gens[si], None)
            if si + 1 < nsp:
                next(gens[si + 1], None)
            next(gens[si], None)
            if si + 2 < nsp:
                issue_x(specs[si + 2], si + 2)
        seen = set()
        for ob in out_owners:
            if id(ob) in seen or ob.k.dsem is None:
                continue
            seen.add(id(ob))
            nc.sync.wait_ge(ob.k.dsem, ob.k.dcnt)
    return nc


_NC_CACHE = {}


def _rope_tables(pos):
    half = 32
    inv = 10000.0 ** (-np.arange(half, dtype=np.float64) / half)
    ang = (pos.astype(np.float32)[None, :] * inv.astype(np.float32)[:, None]).astype(np.float32)
    c = np.cos(ang.astype(np.float64)).astype(np.float32)
    s = np.sin(ang.astype(np.float64)).astype(np.float32)
    return np.concatenate([c, c], axis=0), np.concatenate([s, s], axis=0)


def kernel(x_prompt, x_sample, cache_k, cache_v, state_ssm_re, state_ssm_im, state_conv,
           norm1_g, w_in, attn_sinks, ssm_A_re, ssm_A_im, ssm_log_dt, ssm_B_re, ssm_B_im,
           ssm_C_re, ssm_C_im, ssm_D, w_glu, onorm_attn_g, onorm_ssm_g, w_out, norm2_g,
           w_up, conv_w, conv_b, w_down, final_g):
    f = lambda a: np.ascontiguousarray(np.asarray(a, dtype=np.float32))
    NM = 4096
    n_main = NM // T
    n_pre = (4096 - T) // T
    key = (n_pre, n_main)
    if key not in _NC_CACHE:
        _NC_CACHE[key] = build(n_pre, n_main)
    nc = _NC_CACHE[key]
    shared = dict(
        norm1_g=f(norm1_g[0]), w_in=f(w_in[0]), sinks=f(attn_sinks[0]), A_re=f(ssm_A_re[0]), A_im=f(ssm_A_im[0]),
        log_dt=f(ssm_log_dt[0]), B_re=f(ssm_B_re[0]), B_im=f(ssm_B_im[0]), C_re=f(ssm_C_re[0]), C_im=f(ssm_C_im[0]),
        ssm_D=f(ssm_D[0]), w_glu=f(w_glu[0]), on_a=f(onorm_attn_g[0]), on_s=f(onorm_ssm_g[0]), w_out=f(w_out[0]),
        norm2_g=f(norm2_g[0]), w_up=f(w_up[0]), conv_w=f(conv_w[0]), conv_b=f(conv_b[0]), w_down=f(w_down[0]),
        final_g=f(final_g), ident=np.eye(128, dtype=np.float32))
    half_first = np.zeros(128, np.float32); half_first[:64] = NEG
    half_second = np.zeros(128, np.float32); half_second[64:] = NEG
    full = np.full(128, NEG, np.float32)
    samp_m = np.zeros(128, np.float32); samp_m[16:] = NEG
    in_maps = []
    for c in range(8):
        seq, hf = c // 2, c % 2
        m = dict(shared)
        if hf == 0:
            m["xprev"] = np.zeros((4096, D), np.float32)
            m["maskc"] = np.stack([full, full, half_first, half_second, samp_m], axis=1)
            base = 0
        else:
            m["xprev"] = f(x_prompt[seq, 0:4096])
            m["maskc"] = np.stack([np.zeros(128, np.float32), half_first, half_first, half_second, samp_m], axis=1)
            base = 4096
        m["xmain"] = f(x_prompt[seq, hf * 4096:(hf + 1) * 4096])
        pos = np.concatenate([np.arange(base - T, base + NM), 2048 + np.arange(128), 2048 + np.arange(128)]).astype(np.float64)
        pos = np.maximum(pos, 0)
        rc_, rs_ = _rope_tables(pos)
        m["ropec"] = f(rc_); m["ropes"] = f(rs_)
        m["xs"] = f(x_sample[2 * c:2 * c + 2])
        m["cache_k"] = f(cache_k[0, 2 * c:2 * c + 2]).reshape(2, 128, 128)
        m["cache_v"] = f(cache_v[0, 2 * c:2 * c + 2]).reshape(2, 128, 128)
        m["st_re"] = f(state_ssm_re[0, 2 * c:2 * c + 2]); m["st_im"] = f(state_ssm_im[0, 2 * c:2 * c + 2])
        m["st_conv"] = f(state_conv[0, 2 * c:2 * c + 2])
        in_maps.append(m)
    res = run_bass_kernel_spmd(nc, in_maps, core_ids=list(range(8))).results
    y_prompt = np.zeros((4, 8192, D), np.float32)
    kp = np.zeros((1, 4, 128, 2, 64), np.float32); vp = np.zeros_like(kp)
    rp = np.zeros((1, 4, 32, 64), np.float32); ip = np.zeros_like(rp)
    cp = np.zeros((1, 4, 2, 5632), np.float32)
    ys = np.zeros((16, 16, D), np.float32)
    ksm = np.zeros((1, 16, 128, 2, 64), np.float32); vsm = np.zeros_like(ksm)
    rsm = np.zeros((1, 16, 32, 64), np.float32); ism = np.zeros_like(rsm)
    csm = np.zeros((1, 16, 2, 5632), np.float32)
    for c in range(8):
        r = res[c]
        seq, hf = c // 2, c % 2
        y_prompt[seq, hf * 4096:(hf + 1) * 4096] = r["y_main"]
        if hf == 1:
            kp[0, seq] = r["k_out"].reshape(128, 2, 64); vp[0, seq] = r["v_out"].reshape(128, 2, 64)
            rp[0, seq] = r["sre_out"]; ip[0, seq] = r["sim_out"]; cp[0, seq] = r["conv_out"]
        ys[2 * c:2 * c + 2] = r["y_s"]
        ksm[0, 2 * c:2 * c + 2] = r["ks_out"].reshape(2, 128, 2, 64); vsm[0, 2 * c:2 * c + 2] = r["vs_out"].reshape(2, 128, 2, 64)
        rsm[0, 2 * c:2 * c + 2] = r["sres_out"]; ism[0, 2 * c:2 * c + 2] = r["sims_out"]; csm[0, 2 * c:2 * c + 2] = r["convs_out"]
    return (y_prompt, ys, kp, vp, rp, ip, cp, ksm, vsm, rsm, ism, csm)
```

```python
import contextlib
import os
KSTOP = int(os.environ.get('KSTOP', '99'))
KS2 = int(os.environ.get('KS2', '99'))
import math
import numpy as np
import concourse.bass as bass
import concourse.mybir as mybir
from concourse.bass_utils import run_bass_kernel_spmd

F32 = mybir.dt.float32
BF16 = mybir.dt.bfloat16
ALU = mybir.AluOpType
AF = mybir.ActivationFunctionType
AX = mybir.AxisListType

D = 1024
T = 256
NHID = 22
EPS = 1e-6
SEG = 256
NEG = -30000.0
TWO_PI = 2.0 * math.pi


class Tk:
    def __init__(self, name):
        self.name = name
        self.w = None
        self.r = {}
        self.dsem = None
        self.dcnt = 0


class B:
    def __init__(self, t, name):
        self.t = t
        self.k = Tk(name)

    def __getitem__(self, key):
        return self.t[key]


class Ctx:
    def __init__(self, nc, es):
        self.nc = nc
        self.es = es
        self.e = dict(pe=nc.tensor, act=nc.scalar, dve=nc.vector, pool=nc.gpsimd, sp=nc.sync)
        self.sem = {k: es.enter_context(nc.semaphore("s_" + k)) for k in ["pe", "act", "dve", "pool"]}
        self.cnt = {k: 0 for k in self.sem}
        self.waited = {k: {} for k in self.e}
        self.pend = {k: ([], []) for k in self.e}
        self.nds = 0
        self.nt = 0
        self.cur_es = es
        self.dma_owners = []

    def barrier(self):
        for e in self.e:
            for k in self.sem:
                if self.cnt[k] > 0:
                    self.e[e].wait_ge(self.sem[k], self.cnt[k])
                    self.waited[e][k] = self.cnt[k]
            for tk in self.dma_owners:
                self.e[e].wait_ge(tk.dsem, tk.dcnt)

    def sb(self, shape, dt, name=None):
        self.nt += 1
        name = "sb_" + (name or f"t{self.nt}")
        return B(self.cur_es.enter_context(self.nc.sbuf_tensor(name, list(shape), dt)), name)

    def ps(self, name):
        return B(self.es.enter_context(self.nc.psum_tensor(name, [128, 512], F32)), name)

    def _deps(self, eng, reads, writes):
        deps = {}

        def add(d, same_ok):
            if d is None:
                return
            key, h, v = d
            if key == eng and not same_ok and eng == "pe":
                return
            if key not in deps or deps[key][1] < v:
                deps[key] = (h, v)
        for b in reads:
            add(b.k.w, True)
        for b in writes:
            add(b.k.w, False)
            for d in b.k.r.values():
                add(d, False)
        for key, (h, v) in deps.items():
            if self.waited[eng].get(key, 0) >= v:
                continue
            self.e[eng].wait_ge(h, v)
            self.waited[eng][key] = v

    def op(self, eng, fn, r=(), w=(), inc=True):
        self._deps(eng, r, w)
        ins = fn(self.e[eng])
        pr, pw = self.pend[eng]
        pr.extend(r)
        pw.extend(w)
        if inc:
            self.cnt[eng] += 1
            v = self.cnt[eng]
            ins.then_inc(self.sem[eng], 1)
            d = (eng, self.sem[eng], v)
            for b in pr:
                b.k.r[eng] = d
            for b in pw:
                b.k.w = d
                b.k.r = {}
            self.pend[eng] = ([], [])
        return ins

    def dma(self, out, in_, r=(), w=(), owner=None, slow=False):
        self._deps("sp", r, w)
        ow = (owner or (w[0] if w else r[0])).k
        if ow.dsem is None:
            ow.dsem = self.es.enter_context(self.nc.semaphore(f"d{self.nds}"))
            self.nds += 1
            self.dma_owners.append(ow)
        ow.dcnt += 16
        if slow:
            ins = self.nc.sync.dma_start(out=out, in_=in_, allow_slow_non_contiguous=True)
        else:
            ins = self.nc.sync.dma_start(out=out, in_=in_)
        ins.then_inc(ow.dsem, 16)
        d = ("D" + ow.name, ow.dsem, ow.dcnt)
        for b in r:
            b.k.r[d[0]] = d
        for b in w:
            b.k.w = d
            b.k.r = {}
        return ow


class DR:
    def __init__(self, ap, name):
        self.ap = ap
        self.k = Tk(name)


def build(n_pre, n_main, do_sample=True):
    nc = bass.Bass("TRN2", target_bir_lowering=False)
    NP = n_pre * T
    NM = n_main * T

    def din(name, shape):
        return nc.dram_tensor(name, list(shape), F32, kind="ExternalInput").ap()

    def dout(name, shape):
        return nc.dram_tensor(name, list(shape), F32, kind="ExternalOutput").ap()

    xprev = din("xprev", [NP + T, D])
    xmain = din("xmain", [NM, D])
    xs = din("xs", [2, 16, D])
    cache_k = din("cache_k", [2, 128, 128])
    cache_v = din("cache_v", [2, 128, 128])
    st_re = din("st_re", [2, 32, 64])
    st_im = din("st_im", [2, 32, 64])
    st_conv = din("st_conv", [2, 2, 5632])
    norm1_g = din("norm1_g", [D])
    w_in = din("w_in", [D, 1280])
    sinks = din("sinks", [8])
    A_re = din("A_re", [32, 64])
    A_im = din("A_im", [32, 64])
    log_dt = din("log_dt", [32])
    B_re = din("B_re", [32, 64, 16])
    B_im = din("B_im", [32, 64, 16])
    C_re = din("C_re", [32, 16, 64])
    C_im = din("C_im", [32, 16, 64])
    ssm_D = din("ssm_D", [32, 16])
    w_glu = din("w_glu", [512, 512])
    on_a = din("on_a", [512])
    on_s = din("on_s", [512])
    w_out = din("w_out", [D, D])
    norm2_g = din("norm2_g", [D])
    w_up = din("w_up", [D, 5632])
    conv_w = din("conv_w", [3, 5632])
    conv_b = din("conv_b", [5632])
    w_down = din("w_down", [2816, D])
    final_g = din("final_g", [D])
    ident_in = din("ident", [128, 128])
    maskc_in = din("maskc", [128, 5])
    ropec_in = din("ropec", [64, T + NM + 256])
    ropes_in = din("ropes", [64, T + NM + 256])

    y_main = dout("y_main", [NM, D])
    y_s = dout("y_s", [2, 16, D])
    k_out = dout("k_out", [128, 128])
    v_out = dout("v_out", [128, 128])
    sre_out = dout("sre_out", [32, 64])
    sim_out = dout("sim_out", [32, 64])
    conv_out = dout("conv_out", [2, 5632])
    ks_out = dout("ks_out", [2, 128, 128])
    vs_out = dout("vs_out", [2, 128, 128])
    sres_out = dout("sres_out", [2, 32, 64])
    sims_out = dout("sims_out", [2, 32, 64])
    convs_out = dout("convs_out", [2, 2, 5632])

    wup_scr = DR(nc.dram_tensor("wup_scr", [NHID, 128, 8 * 256], BF16, kind="Internal").ap(), "wup_scr")
    wdn_scr = DR(nc.dram_tensor("wdn_scr", [NHID, 128, 1024], BF16, kind="Internal").ap(), "wdn_scr")

    with contextlib.ExitStack() as es:
        cx = Ctx(nc, es)
        sb = cx.sb
        op = cx.op
        dma = cx.dma
        out_owners = []

        banks = [cx.ps(f"ps{i}") for i in range(8)]
        bank_i = [0]

        def nb():
            b = banks[bank_i[0] % 8]
            bank_i[0] += 1
            return b

        ident_f = sb([128, 128], F32, "ident_f")
        ident_b = sb([128, 128], BF16, "ident_b")
        maskc = sb([128, 5], F32, "maskc")
        pi_c = sb([128, 1], F32, "pi_c")
        eps_c = sb([128, 1], F32, "eps_c")
        eps4_c = sb([128, 1], F32, "eps4_c")
        ones_b = sb([128, 128], BF16, "ones_b")
        g1c = sb([128, 8], F32, "g1c")
        g2c = sb([128, 8], F32, "g2c")
        goc = sb([128, 8], F32, "goc")
        GF = sb([128, D], F32, "GF")
        ES = sb([128, 8], F32, "ES")
        cw = sb([128, 3, 44], F32, "cw")
        cb = sb([128, 44], F32, "cb")
        Dcol = sb([128, 4], F32, "Dcol")
        Win = sb([128, 8, 1920], BF16, "Win")
        Wout = sb([128, 8, 1024], BF16, "Wout")
        Wglu = sb([128, 4, 512], BF16, "Wglu")
        Bz = sb([128, 16, 2, 128], BF16, "Bz")
        Cz = sb([128, 16, 2, 32], BF16, "Cz")
        Ec = sb([128, 16, SEG], F32, "Ec")
        Es = sb([128, 16, SEG], F32, "Es")
        rmag = sb([128, 16], F32, "rmag")
        TAIL = sb([128, 44, 2], F32, "TAIL")
        Hc = sb([128, 2, 16], F32, "Hc")
        es2 = contextlib.ExitStack()
        cx.cur_es = es2
        NSTG = 8
        stage = [sb([128, 1280], F32, f"stage{i}") for i in range(NSTG)]
        stb = [sb([128, 1024], BF16, f"stb{i}") for i in range(NSTG)]

        dma(ident_f[:], ident_in, w=[ident_f])
        dma(maskc[:], maskc_in, w=[maskc])
        dma(g1c[:], norm1_g.rearrange("(k p) -> p k", p=128), w=[g1c], slow=True)
        dma(g2c[:], norm2_g.rearrange("(k p) -> p k", p=128), w=[g2c], slow=True)
        dma(goc[:, 0:4], on_a.rearrange("(k p) -> p k", p=128), w=[goc], slow=True)
        dma(goc[:, 4:8], on_s.rearrange("(k p) -> p k", p=128), w=[goc], slow=True)
        dma(GF[:], final_g.partition_broadcast(128), w=[GF])
        dma(ES[:], sinks.partition_broadcast(128), w=[ES])
        for j in range(3):
            dma(cw[:, j, :], conv_w[j].rearrange("(c p) -> p c", p=128), w=[cw], slow=True)
        dma(cb[:], conv_b.rearrange("(c p) -> p c", p=128), w=[cb], slow=True)
        dma(Dcol[:], ssm_D.rearrange("(k g) c -> (g c) k", g=8), w=[Dcol], slow=True)

        op("pool", lambda e: e.memset(pi_c[:], math.pi), w=[pi_c])
        op("pool", lambda e: e.memset(eps_c[:], EPS), w=[eps_c])
        op("pool", lambda e: e.memset(eps4_c[:], 4 * EPS), w=[eps4_c])
        op("pool", lambda e: e.memset(ones_b[:], 1.0), w=[ones_b])
        op("pool", lambda e: e.memset(TAIL[:], 0.0), w=[TAIL])
        op("pool", lambda e: e.memset(Hc[:], 0.0), w=[Hc])
        op("dve", lambda e: e.tensor_copy(out=ident_b[:], in_=ident_f[:]), r=[ident_f], w=[ident_b])
        op("act", lambda e: e.activation(out=ES[:], in_=ES[:], func=AF.Exp), r=[ES], w=[ES])

        def small(name, shape=(128, 16)):
            return sb(list(shape), F32, name)
        Are = small("Are"); Aim = small("Aim"); Ldt = small("Ldt")
        dma(Are[:], A_re.rearrange("(q gl) p -> (gl p) q", gl=2), w=[Are], slow=True)
        dma(Aim[:], A_im.rearrange("(q gl) p -> (gl p) q", gl=2), w=[Aim], slow=True)
        ldv = log_dt.rearrange("(q gl) -> gl q", gl=2)
        for gl in range(2):
            dma(Ldt[gl * 64:(gl + 1) * 64, :], ldv[gl].partition_broadcast(64), w=[Ldt], slow=True)
        Bre = sb([128, 16, 16], F32, "Bre"); Bim = sb([128, 16, 16], F32, "Bim")
        dma(Bre[:], B_re.rearrange("(q gl) p c -> (gl p) q c", gl=2), w=[Bre])
        dma(Bim[:], B_im.rearrange("(q gl) p c -> (gl p) q c", gl=2), w=[Bim])
        Cn = [sb([16, 128], F32, "Cn0"), sb([16, 128], F32, "Cn1")]
        Cv = [Csrc.rearrange("(q gl) c p -> q c gl p", gl=2) for Csrc in (C_re, C_im)]

        dt_ = small("dt_"); ar = small("ar"); ai = small("ai"); xx = small("xx"); xc = small("xc")
        cth = small("cth"); sth = small("sth"); lre = small("lre"); lim = small("lim")
        t1 = small("t1"); t2 = small("t2"); den = small("den"); kre = small("kre"); kim = small("kim")
        V = "dve"
        op("act", lambda e: e.activation(out=dt_[:], in_=Ldt[:], func=AF.Exp), r=[Ldt], w=[dt_])
        op(V, lambda e: e.tensor_tensor(out=ar[:], in0=dt_[:], in1=Are[:], op=ALU.mult), r=[dt_, Are], w=[ar])
        op(V, lambda e: e.tensor_tensor(out=ai[:], in0=dt_[:], in1=Aim[:], op=ALU.mult), r=[dt_, Aim], w=[ai])
        op("act", lambda e: e.activation(out=rmag[:], in_=ar[:], func=AF.Exp), r=[ar], w=[rmag])
        ti_ = sb([128, 16], mybir.dt.int32, "ti_")
        tf_ = small("tf_")

        def reduce_2pi(dst, src, shift):
            op(V, lambda e: e.tensor_scalar(out=dst[:], in0=src[:], scalar1=shift, scalar2=None, op0=ALU.add), r=[src], w=[dst])
            op(V, lambda e: e.tensor_scalar(out=ti_[:], in0=dst[:], scalar1=1.0 / TWO_PI, scalar2=None, op0=ALU.mult), r=[dst], w=[ti_])
            op(V, lambda e: e.tensor_copy(out=tf_[:], in_=ti_[:]), r=[ti_], w=[tf_])
            op(V, lambda e: e.scalar_tensor_tensor(out=dst[:], in0=tf_[:], scalar=-TWO_PI, in1=dst[:], op0=ALU.mult, op1=ALU.add), r=[tf_, dst], w=[dst])
            op(V, lambda e: e.tensor_scalar(out=tf_[:], in0=dst[:], scalar1=0.0, scalar2=None, op0=ALU.is_lt), r=[dst], w=[tf_])
            op(V, lambda e: e.scalar_tensor_tensor(out=dst[:], in0=tf_[:], scalar=TWO_PI, in1=dst[:], op0=ALU.mult, op1=ALU.add), r=[tf_, dst], w=[dst])
        reduce_2pi(xx, ai, TWO_PI)
        reduce_2pi(xc, xx, 0.5 * math.pi)
        op("act", lambda e: e.activation(out=sth[:], in_=xx[:], func=AF.Sin, bias=pi_c[:], scale=-1.0), r=[xx, pi_c], w=[sth])
        op("act", lambda e: e.activation(out=cth[:], in_=xc[:], func=AF.Sin, bias=pi_c[:], scale=-1.0), r=[xc, pi_c], w=[cth])
        op(V, lambda e: e.tensor_tensor(out=lre[:], in0=rmag[:], in1=cth[:], op=ALU.mult), r=[rmag, cth], w=[lre])
        op(V, lambda e: e.tensor_tensor(out=lim[:], in0=rmag[:], in1=sth[:], op=ALU.mult), r=[rmag, sth], w=[lim])
        op(V, lambda e: e.tensor_scalar(out=t1[:], in0=lre[:], scalar1=-1.0, scalar2=None, op0=ALU.add), r=[lre], w=[t1])
        op(V, lambda e: e.tensor_tensor(out=den[:], in0=Are[:], in1=Are[:], op=ALU.mult), r=[Are], w=[den])
        op(V, lambda e: e.tensor_tensor(out=t2[:], in0=Aim[:], in1=Aim[:], op=ALU.mult), r=[Aim], w=[t2])
        op(V, lambda e: e.tensor_tensor(out=den[:], in0=den[:], in1=t2[:], op=ALU.add), r=[den, t2], w=[den])
        op(V, lambda e: e.reciprocal(out=den[:], in_=den[:]), r=[den], w=[den])
        op(V, lambda e: e.tensor_tensor(out=kre[:], in0=t1[:], in1=Are[:], op=ALU.mult), r=[t1, Are], w=[kre])
        op(V, lambda e: e.tensor_tensor(out=t2[:], in0=lim[:], in1=Aim[:], op=ALU.mult), r=[lim, Aim], w=[t2])
        op(V, lambda e: e.tensor_tensor(out=kre[:], in0=kre[:], in1=t2[:], op=ALU.add), r=[kre, t2], w=[kre])
        op(V, lambda e: e.tensor_tensor(out=kre[:], in0=kre[:], in1=den[:], op=ALU.mult), r=[kre, den], w=[kre])
        op(V, lambda e: e.tensor_tensor(out=kim[:], in0=lim[:], in1=Are[:], op=ALU.mult), r=[lim, Are], w=[kim])
        op(V, lambda e: e.tensor_tensor(out=t2[:], in0=t1[:], in1=Aim[:], op=ALU.mult), r=[t1, Aim], w=[t2])
        op(V, lambda e: e.tensor_tensor(out=kim[:], in0=kim[:], in1=t2[:], op=ALU.subtract), r=[kim, t2], w=[kim])
        op(V, lambda e: e.tensor_tensor(out=kim[:], in0=kim[:], in1=den[:], op=ALU.mult), r=[kim, den], w=[kim])
        Bbr = sb([128, 16, 16], F32, "Bbr"); Bbi = sb([128, 16, 16], F32, "Bbi"); Btm = sb([128, 16, 16], F32, "Btm")
        kreb = kre[:, :, None].broadcast_to([128, 16, 16])
        kimb = kim[:, :, None].broadcast_to([128, 16, 16])
        op(V, lambda e: e.tensor_tensor(out=Bbr[:], in0=Bre[:], in1=kreb, op=ALU.mult), r=[Bre, kre], w=[Bbr])
        op(V, lambda e: e.tensor_tensor(out=Btm[:], in0=Bim[:], in1=kimb, op=ALU.mult), r=[Bim, kim], w=[Btm])
        op(V, lambda e: e.tensor_tensor(out=Bbr[:], in0=Bbr[:], in1=Btm[:], op=ALU.subtract), r=[Bbr, Btm], w=[Bbr])
        op(V, lambda e: e.tensor_tensor(out=Bbi[:], in0=Bim[:], in1=kreb, op=ALU.mult), r=[Bim, kre], w=[Bbi])
        op(V, lambda e: e.tensor_tensor(out=Btm[:], in0=Bre[:], in1=kimb, op=ALU.mult), r=[Bre, kim], w=[Btm])
        op(V, lambda e: e.tensor_tensor(out=Bbi[:], in0=Bbi[:], in1=Btm[:], op=ALU.add), r=[Bbi, Btm], w=[Bbi])
        Zq = [sb([128, 128], F32, "Zq0"), sb([128, 128], F32, "Zq1")]
        zi = 0
        for q in range(16):
            qm = q % 4
            for ri, Bb in enumerate((Bbr, Bbi)):
                z = Zq[zi % 2]
                zi += 1
                op("pool", lambda e: e.memset(z[:], 0.0), w=[z])
                op("pool", lambda e: e.tensor_copy(out=z[0:64, 32 * qm:32 * qm + 16], in_=Bb[0:64, q, :]), r=[Bb], w=[z])
                op("pool", lambda e: e.tensor_copy(out=z[64:128, 32 * qm + 16:32 * qm + 32], in_=Bb[64:128, q, :]), r=[Bb, z], w=[z])
                pb = nb()
                op("pe", lambda e: e.transpose(out=pb[:, 0:128], in_=z[:], identity=ident_f[:]), r=[z, ident_f], w=[pb])
                op("act", lambda e: e.copy(out=Bz[:, q, ri, :], in_=pb[:, 0:128]), r=[pb], w=[Bz])
        op("pool", lambda e: e.memset(Cz[:], 0.0), w=[Cz])
        for q in range(16):
            for ri in range(2):
                pb = nb()
                dma(Cn[ri][:].rearrange("c (gl p) -> c gl p", gl=2), Cv[ri][q], w=[Cn[ri]])
                op("pe", lambda e: e.transpose(out=pb[:, 0:16], in_=Cn[ri][:, :], identity=ident_f[0:16, 0:16]), r=[Cn[ri], ident_f], w=[pb])
                sc = 1.0 if ri == 0 else -1.0
                op("act", lambda e: e.mul(out=Cz[0:64, q, ri, 0:16], in_=pb[0:64, 0:16], mul=sc), r=[pb], w=[Cz])
                op("act", lambda e: e.mul(out=Cz[64:128, q, ri, 16:32], in_=pb[64:128, 0:16], mul=sc), r=[pb, Cz], w=[Cz])
        op("pool", lambda e: e.memset(Ec[:, :, 0:1], 1.0), w=[Ec])
        op("pool", lambda e: e.memset(Es[:, :, 0:1], 0.0), w=[Es])
        op("pool", lambda e: e.tensor_copy(out=Ec[:, :, 1], in_=cth[:]), r=[cth, Ec], w=[Ec])
        op("pool", lambda e: e.tensor_copy(out=Es[:, :, 1], in_=sth[:]), r=[sth, Es], w=[Es])
        ta = sb([128, 16, SEG // 2], F32, "ta"); tb_ = sb([128, 16, SEG // 2], F32, "tb_")
        n = 1
        while n < SEG:
            if n > 1:
                a, b_, c, d_ = Ec[:, :, n - 1], Es[:, :, n - 1], Ec[:, :, 1], Es[:, :, 1]
                op(V, lambda e: e.tensor_tensor(out=ta[:, :, 0], in0=a, in1=c, op=ALU.mult), r=[Ec], w=[ta])
                op(V, lambda e: e.tensor_tensor(out=tb_[:, :, 0], in0=b_, in1=d_, op=ALU.mult), r=[Es], w=[tb_])
                op(V, lambda e: e.tensor_tensor(out=Ec[:, :, n], in0=ta[:, :, 0], in1=tb_[:, :, 0], op=ALU.subtract), r=[ta, tb_, Ec], w=[Ec])
                op(V, lambda e: e.tensor_tensor(out=ta[:, :, 0], in0=a, in1=d_, op=ALU.mult), r=[Ec, Es], w=[ta])
                op(V, lambda e: e.tensor_tensor(out=tb_[:, :, 0], in0=b_, in1=c, op=ALU.mult), r=[Es, Ec], w=[tb_])
                op(V, lambda e: e.tensor_tensor(out=Es[:, :, n], in0=ta[:, :, 0], in1=tb_[:, :, 0], op=ALU.add), r=[ta, tb_, Es], w=[Es])
            if n > 1:
                m = n - 1
                cn = Ec[:, :, n:n + 1].broadcast_to([128, 16, m])
                sn = Es[:, :, n:n + 1].broadcast_to([128, 16, m])
                op(V, lambda e: e.tensor_tensor(out=ta[:, :, 0:m], in0=Ec[:, :, 1:n], in1=cn, op=ALU.mult), r=[Ec], w=[ta])
                op(V, lambda e: e.tensor_tensor(out=tb_[:, :, 0:m], in0=Es[:, :, 1:n], in1=sn, op=ALU.mult), r=[Es], w=[tb_])
                op(V, lambda e: e.tensor_tensor(out=Ec[:, :, n + 1:2 * n], in0=ta[:, :, 0:m], in1=tb_[:, :, 0:m], op=ALU.subtract), r=[ta, tb_, Ec], w=[Ec])
                op(V, lambda e: e.tensor_tensor(out=ta[:, :, 0:m], in0=Ec[:, :, 1:n], in1=sn, op=ALU.mult), r=[Ec, Es], w=[ta])
                op(V, lambda e: e.tensor_tensor(out=tb_[:, :, 0:m], in0=Es[:, :, 1:n], in1=cn, op=ALU.mult), r=[Es, Ec], w=[tb_])
                op(V, lambda e: e.tensor_tensor(out=Es[:, :, n + 1:2 * n], in0=ta[:, :, 0:m], in1=tb_[:, :, 0:m], op=ALU.add), r=[ta, tb_, Es], w=[Es])
            n *= 2

        for kt in range(8):
            st = stage[kt % NSTG]
            dma(st[:], w_in[kt * 128:(kt + 1) * 128, :], w=[st])
            g = g1c[:, kt:kt + 1]
            eng = "dve"
            op(eng, lambda e: e.tensor_scalar(out=Win[:, kt, 0:512], in0=st[:, 0:512], scalar1=g, scalar2=None, op0=ALU.mult),
               r=[st, g1c], w=[Win])
            op(eng, lambda e: e.tensor_scalar(out=Win[:, kt, 1024:1152], in0=st[:, 512:640], scalar1=g, scalar2=None, op0=ALU.mult),
               r=[st, g1c], w=[Win])
            op(eng, lambda e: e.tensor_scalar(out=Win[:, kt, 1280:1792], in0=st[:, 768:1280], scalar1=g, scalar2=None, op0=ALU.mult),
               r=[st, g1c], w=[Win])
            op(eng, lambda e: e.tensor_scalar(out=Win[:, kt, 1792:1920], in0=st[:, 640:768], scalar1=g, scalar2=None, op0=ALU.mult),
               r=[st, g1c], w=[Win])
            for (dst0, src0, nh) in ((512, 0, 8), (1152, 512, 2)):
                dv = Win[:, kt, dst0:dst0 + nh * 64].rearrange("p (h two d) -> p h two d", two=2, d=32)
                sv = st[:, src0:src0 + nh * 64].rearrange("p (h two d) -> p h two d", two=2, d=32)
                op(eng, lambda e: e.tensor_scalar(out=dv[:, :, 0, :], in0=sv[:, :, 1, :], scalar1=g, scalar2=-1.0,
                                                  op0=ALU.mult, op1=ALU.mult), r=[st, g1c], w=[Win])
                op(eng, lambda e: e.tensor_scalar(out=dv[:, :, 1, :], in0=sv[:, :, 0, :], scalar1=g, scalar2=None,
                                                  op0=ALU.mult), r=[st, g1c], w=[Win])
        for kt in range(8):
            st = stage[kt % NSTG]
            dma(st[:, 0:1024], w_out[kt * 128:(kt + 1) * 128, :], w=[st])
            op("dve",
               lambda e: e.tensor_scalar(out=Wout[:, kt, :], in0=st[:, 0:1024], scalar1=goc[:, kt:kt + 1], scalar2=None, op0=ALU.mult),
               r=[st, goc], w=[Wout])
        for kt in range(4):
            st = stage[(kt + 4) % NSTG]
            dma(st[:, 0:512], w_glu[kt * 128:(kt + 1) * 128, :], w=[st])
            op("act", lambda e: e.copy(out=Wglu[:, kt, :], in_=st[:, 0:512]), r=[st], w=[Wglu])
        wup_v = wup_scr.ap.rearrange("j p (k gv c) -> j p k gv c", k=8, gv=2)
        it = 0
        for kt in range(8):
            for gv in range(2):
                for cblk in range(3):
                    c0 = cblk * 1024
                    ncol = min(1024, 2816 - c0)
                    st = stage[it % NSTG]
                    sbf = stb[it % NSTG]
                    dma(st[:, 0:ncol], w_up[kt * 128:(kt + 1) * 128, gv * 2816 + c0: gv * 2816 + c0 + ncol], w=[st])
                    op("dve",
                       lambda e: e.tensor_scalar(out=sbf[:, 0:ncol], in0=st[:, 0:ncol], scalar1=g2c[:, kt:kt + 1], scalar2=None, op0=ALU.mult),
                       r=[st, g2c], w=[sbf])
                    j0 = c0 // 128
                    nj = ncol // 128
                    dma(wup_v[j0:j0 + nj, :, kt, gv, :].rearrange("j p c -> p j c"),
                        sbf[:, 0:ncol].rearrange("p (j c) -> p j c", c=128), r=[sbf], w=[wup_scr], owner=sbf)
                    it += 1
        for j in range(NHID):
            st = stage[it % NSTG]
            sbf = stb[it % NSTG]
            dma(st[:, 0:1024], w_down[j * 128:(j + 1) * 128, :], w=[st])
            op("dve",
               lambda e: e.tensor_scalar(out=sbf[:, :], in0=st[:, 0:1024], scalar1=0.5, scalar2=None, op0=ALU.mult),
               r=[st], w=[sbf])
            dma(wdn_scr.ap[j], sbf[:, :], r=[sbf], w=[wdn_scr], owner=sbf)
            it += 1

        cx.barrier()
        es2.close()
        cx.cur_es = es
        X = [sb([128, 2, D], F32, f"X{i}") for i in range(2)]
        junk = sb([128, D], BF16, "junk")
        hn = sb([128, D], BF16, "hn")
        ss = sb([128, 4], F32, "ss")
        hT = sb([128, 8, T], BF16, "hT")
        qT = sb([64, 8, T], BF16, "qT")
        kT = [sb([64, 2, 128], BF16, f"kT{i}") for i in range(3)]
        kTf = sb([64, 2, 128], F32, "kTf")
        Vt = [sb([128, 2, 65], BF16, f"Vt{i}") for i in range(3)]
        Vf = sb([128, 128], F32, "Vf")
        Kf = sb([128, 128], F32, "Kf")
        uT = sb([128, 4, T], BF16, "uT")
        rc = [sb([64, T], F32, "rc0")] * 2
        rs = [sb([64, T], F32, "rs0")] * 2
        PT = sb([128, 4, 2, 128], BF16, "PT")
        attn = sb([128, 512], F32, "attn")
        dn = sb([128, 8], F32, "dn")
        mT = hT
        sc_ab = sb([128, 2, T], F32, "sc_ab")
        sc_a = B(sc_ab.t[:, 0, :], "sc_a"); sc_a.k = sc_ab.k
        sc_b = B(sc_ab.t[:, 1, :], "sc_b"); sc_b.k = sc_ab.k
        SB0 = [sb([128, 2, T], F32, f"SB0{i}") for i in range(2)]
        SB1 = [sb([128, 2, T], F32, f"SB1{i}") for i in range(2)]
        ST0 = [sb([128, 2, T], F32, f"ST0{i}") for i in range(2)]
        ST1 = [sb([128, 2, T], F32, "ST10")] * 2
        ST2 = [sb([128, 2, T], F32, f"ST2{i}") for i in range(2)]
        ST3 = [sb([128, 2, T], F32, "ST30")] * 2
        HBR = [sb([128, 2, T], BF16, f"HBR{i}") for i in range(2)]
        HBI = [sb([128, 2, T], BF16, f"HBI{i}") for i in range(2)]
        G6 = sb([128, 6, 16], F32, "G6")
        GL = sb([128, 2, 16], F32, "GL")
        W1 = [sb([128, T], F32, "W10")] * 2
        W2 = [sb([128, T], F32, "W20")] * 2
        zf = sb([128, 4, T], F32, "zf")
        zb = sb([128, 4, T], BF16, "zb")
        s2 = zf
        sqb = B(junk.t[:].rearrange("p (c t) -> p c t", c=4), "sqb_alias")
        sqb.k = junk.k
        w2 = W2[0]
        rstd_s = sb([128, T], F32, "rstd_s")
        xT2 = hT
        def alias(parent, ap, name):
            b_ = B(ap, name)
            b_.k = parent.k
            return b_

        def flat(bt):
            return bt.t[:].rearrange("p a t -> p (a t)")
        UPS = [[alias(SB0[s_], flat(SB0[s_])[:, 0:T + 2], f"UPa{s_}0"), alias(SB1[s_], flat(SB1[s_])[:, 0:T + 2], f"UPa{s_}1")] for s_ in range(2)]
        CGS = [alias(ST0[s_], ST0[s_].t[:, 0, :], f"cg{s_}") for s_ in range(2)]
        CVS = [alias(ST2[s_], ST2[s_].t[:, 0, :], f"cv{s_}") for s_ in range(2)]
        THS = [alias(ST0[s_], ST0[s_].t[:, 1, :], f"th{s_}") for s_ in range(2)]
        actT = sb([128, NHID, T], BF16, "actT")
        NWU = 3
        NWD = 4
        wup_b = [sb([128, 8, 256], BF16, f"wup{i}") for i in range(NWU)]
        wdn_b = [sb([128, 1024], BF16, f"wdn{i}") for i in range(NWD)]
        Yown = [B(None, "yown0"), B(None, "yown1")]
        Vo = B(attn.t[:, 0:128], "Vo_alias"); Vo.k = attn.k
        Ko = B(attn.t[:, 128:256], "Ko_alias"); Ko.k = attn.k
        Ho = sb([128, 2, 16], F32, "Ho"); To = sb([128, 44, 2], F32, "To")
        ring = [0]
        op("pool", lambda e: e.memset(kTf[:], 0.0), w=[kTf])
        for i_ in range(3):
            op("pool", lambda e: e.memset(kT[i_][:], 0.0), w=[kT[i_]])
            op("pool", lambda e: e.memset(Vt[i_][:], 0.0), w=[Vt[i_]])

        def rms_rstd(dst, src_ss, n, eps):
            np_ = dst.shape[0]
            op("act", lambda e: e.activation(out=dst, in_=src_ss, func=AF.Ln, scale=1.0 / n, bias=eps_c[0:np_, :]), r=[ss, eps_c], w=[ss])
            op("act", lambda e: e.activation(out=dst, in_=dst, func=AF.Exp, scale=-0.5), r=[ss], w=[ss])

        def norm_transpose(Xb, nt, ts, dstT):
            for i in range(nt):
                op("act", lambda e: e.activation(out=junk[:ts, :], in_=Xb[:ts, i, :], func=AF.Square, accum_out=ss[:ts, i:i + 1]),
                   r=[Xb], w=[junk, ss])
                rms_rstd(ss[:ts, i:i + 1], ss[:ts, i:i + 1], D, EPS)
                op("dve", lambda e: e.tensor_scalar(out=hn[:ts, :], in0=Xb[:ts, i, :], scalar1=ss[:ts, i:i + 1], scalar2=None, op0=ALU.mult),
                   r=[Xb, ss], w=[hn])
                pb = nb()
                pv = pb[:].bitcast(BF16)
                for kt in range(8):
                    op("pe", lambda e: e.transpose(out=pv[:, kt * 128:kt * 128 + ts], in_=hn[:ts, kt * 128:(kt + 1) * 128], identity=ident_b[:ts, :ts]),
                       r=[hn, ident_b], w=[pb], inc=(kt == 7))
                op("act", lambda e: e.copy(out=dstT[:, :, i * 128:i * 128 + ts],
                                           in_=pv.rearrange("p (k t) -> p k t", k=8)[:, :, 0:ts]), r=[pb], w=[dstT])

        wcnt = dict(ui=0, uu=0, di=0, du=0)

        def up_issue(j):
            wb = wup_b[wcnt["ui"] % NWU]
            wcnt["ui"] += 1
            dma(wb[:].rearrange("p k c -> p (k c)"), wup_scr.ap[j], r=[wup_scr], w=[wb])

        def up_get():
            wb = wup_b[wcnt["uu"] % NWU]
            wcnt["uu"] += 1
            return wb

        def dn_issue(j):
            wd = wdn_b[wcnt["di"] % NWD]
            wcnt["di"] += 1
            dma(wd[:], wdn_scr.ap[j], r=[wdn_scr], w=[wd])

        def dn_get():
            wd = wdn_b[wcnt["du"] % NWD]
            wcnt["du"] += 1
            return wd

        cur_spec = [None, None]

        def issue_x(sp_, bi_):
            if sp_ is None or sp_.get("x_loaded"):
                return
            sp_["x_loaded"] = True
            Xn = X[bi_ % 2]
            if sp_.get("nreal") is None:
                ts_ = min(sp_["Tn"], 128)
                nt_ = (sp_["Tn"] + 127) // 128
                dma(Xn[:ts_, 0:nt_, :], sp_["xsrc"].rearrange("(n p) d -> p n d", p=ts_), w=[Xn])
            else:
                dma(Xn[:sp_["nreal"], 0, :], sp_["xsrc"], w=[Xn])

        def issue_rope(sp_, bi_):
            if sp_ is None or sp_.get("rope_loaded") or sp_["mode"] == "pre":
                return
            sp_["rope_loaded"] = True
            ro = sp_.get("rope_off", 0)
            dma(rc[bi_ % 2][:, 0:sp_["Tn"]], ropec_in[:, ro:ro + sp_["Tn"]], w=[rc[bi_ % 2]])
            dma(rs[bi_ % 2][:, 0:sp_["Tn"]], ropes_in[:, ro:ro + sp_["Tn"]], w=[rs[bi_ % 2]])

        def block(bi, me, nx, xsrc, Tn, mode, rope_off=0, first_main=False, ydst=None, samp=None, last=False, nreal=None):
            ts = min(Tn, 128)
            nt = (Tn + 127) // 128
            Xb = X[bi % 2]
            nr = nreal if nreal is not None else ts
            issue_x(me, bi)
            issue_rope(me, bi)
            sl_win = ring[0]
            if mode != "pre":
                rcb, rsb = rc[bi % 2], rs[bi % 2]
            if mode != "pre":
                for j_ in range(NWU):
                    up_issue(j_)
            norm_transpose(Xb, nt, ts, hT)
            def proj(col0, m, evac):
                pb = nb()
                for kt in range(8):
                    op("pe", lambda e: e.matmul(pb[0:m, 0:Tn], lhsT=Win[:, kt, col0:col0 + m], rhs=hT[:, kt, 0:Tn], start=(kt == 0), stop=(kt == 7)),
                       r=[Win, hT], w=[pb], inc=(kt == 7))
                return pb
            for c in range(4):
                pb = proj(1280 + c * 128, 128, None)
                op("act", lambda e: e.copy(out=uT[:, c, 0:Tn], in_=pb[:, 0:Tn]), r=[pb], w=[uT])
            if mode != "pre":
                for h in range(8):
                    pa = proj(h * 64, 64, None)
                    pr_ = proj(512 + h * 64, 64, None)
                    op("dve", lambda e: e.tensor_tensor(out=sc_a[0:64, 0:Tn], in0=pa[0:64, 0:Tn], in1=rcb[:, 0:Tn], op=ALU.mult), r=[pa, rcb], w=[sc_a])
                    op("dve", lambda e: e.tensor_tensor(out=sc_b[0:64, 0:Tn], in0=pr_[0:64, 0:Tn], in1=rsb[:, 0:Tn], op=ALU.mult), r=[pr_, rsb], w=[sc_b])
                    op("pool", lambda e: e.tensor_tensor(out=qT[:, h, 0:Tn], in0=sc_a[0:64, 0:Tn], in1=sc_b[0:64, 0:Tn], op=ALU.add), r=[sc_a, sc_b], w=[qT])
                slots = [(ring[0] + 1 + i) % 3 for i in range(nt)]
                for g in range(2):
                    pa = proj(1024 + g * 64, 64, None)
                    pr_ = proj(1152 + g * 64, 64, None)
                    op("dve", lambda e: e.tensor_tensor(out=sc_a[0:64, 0:Tn], in0=pa[0:64, 0:Tn], in1=rcb[:, 0:Tn], op=ALU.mult), r=[pa, rcb], w=[sc_a])
                    op("dve", lambda e: e.tensor_tensor(out=sc_b[0:64, 0:Tn], in0=pr_[0:64, 0:Tn], in1=rsb[:, 0:Tn], op=ALU.mult), r=[pr_, rsb], w=[sc_b])
                    for i in range(nt):
                        op("pool", lambda e: e.tensor_tensor(out=kT[slots[i]][:, g, 0:ts], in0=sc_a[0:64, i * 128:i * 128 + ts],
                                                             in1=sc_b[0:64, i * 128:i * 128 + ts], op=ALU.add), r=[sc_a, sc_b], w=[kT[slots[i]]])
                    if last or samp is not None:
                        i = nt - 1
                        op("pool", lambda e: e.tensor_tensor(out=kTf[:, g, 0:ts], in0=sc_a[0:64, i * 128:i * 128 + ts],
                                                             in1=sc_b[0:64, i * 128:i * 128 + ts], op=ALU.add), r=[sc_a, sc_b], w=[kTf])
                for i in range(nt):
                    pb = nb()
                    for kt in range(8):
                        op("pe", lambda e: e.matmul(pb[0:ts, 0:128], lhsT=hT[:, kt, i * 128:i * 128 + ts], rhs=Win[:, kt, 1792:1920], start=(kt == 0), stop=(kt == 7)),
                           r=[Win, hT], w=[pb], inc=(kt == 7))
                    vs_ = Vt[slots[i]]
                    op("pool", lambda e: e.memset(vs_[:, :, 64:65], 1.0), w=[vs_])
                    op("act", lambda e: e.copy(out=vs_[0:ts, :, 0:64], in_=pb[0:ts, 0:128].rearrange("p (g d) -> p g d", g=2)), r=[pb, vs_], w=[vs_])
                    if (last or samp is not None) and i == nt - 1:
                        op("dve", lambda e: e.tensor_copy(out=Vo[0:ts, :], in_=pb[0:ts, 0:128]), r=[pb], w=[Vo])
                        for g in range(2):
                            pk = nb()
                            op("pe", lambda e: e.transpose(out=pk[0:128, 0:64], in_=kTf[:, g, 0:128], identity=ident_f[0:64, 0:64]), r=[kTf, ident_f], w=[pk])
                            op("act", lambda e: e.copy(out=Ko[0:ts, g * 64:(g + 1) * 64], in_=pk[0:ts, 0:64]), r=[pk], w=[Ko])
                        if samp is not None:
                            dma(ks_out[samp, 112:128, :], Ko[0:16, :], r=[Ko], owner=Ko)
                            dma(vs_out[samp, 112:128, :], Vo[0:16, :], r=[Vo], owner=Vo)
                        else:
                            dma(k_out, Ko[:, :], r=[Ko], owner=Ko)
                            dma(v_out, Vo[:, :], r=[Vo], owner=Vo)
                        out_owners.extend([Ko, Vo])
                yield
                if samp is not None:
                    dma(Hc[:, 0, :], st_re[samp].rearrange("(q gl) p -> (gl p) q", gl=2), w=[Hc], slow=True)
                    dma(Hc[:, 1, :], st_im[samp].rearrange("(q gl) p -> (gl p) q", gl=2), w=[Hc], slow=True)
                    for r_ in range(2):
                        dma(TAIL[:, :, r_], st_conv[samp, r_].rearrange("(c p) -> p c", p=128), w=[TAIL], slow=True)
                    sl = sl_win
                    dma(Kf[:], cache_k[samp], w=[Kf])
                    dma(Vf[:], cache_v[samp], w=[Vf])
                    for g in range(2):
                        pb = nb()
                        op("pe", lambda e: e.transpose(out=pb[0:64, 0:128], in_=Kf[:, g * 64:(g + 1) * 64], identity=ident_f[:]), r=[Kf, ident_f], w=[pb])
                        op("act", lambda e: e.copy(out=kT[sl][:, g, :], in_=pb[0:64, 0:128]), r=[pb], w=[kT[sl]])
                    op("pool", lambda e: e.memset(Vt[sl][:, :, 64:65], 1.0), w=[Vt[sl]])
                    op("dve", lambda e: e.tensor_copy(out=Vt[sl][:, :, 0:64], in_=Vf[:].rearrange("p (g d) -> p g d", g=2)), r=[Vf, Vt[sl]], w=[Vt[sl]])
                    dma(ks_out[samp, 0:112, :], Kf[16:128, :], r=[Kf], owner=Kf)
                    dma(vs_out[samp, 0:112, :], Vf[16:128, :], r=[Vf], owner=Vf)
                    out_owners.extend([Kf, Vf])
                def attn_unit(i, g):
                    sl_prev = (slots[i] + 2) % 3
                    sl_cur = slots[i]
                    p0 = nb(); p1 = nb()
                    pss = (p0, p1)
                    for hh in range(4):
                        h = 4 * g + hh
                        for kt_, slk, nk in ((0, sl_prev, 128), (1, sl_cur, ts)):
                            op("pe", lambda e: e.matmul(pss[kt_][0:nk, hh * 128:hh * 128 + ts], lhsT=kT[slk][:, g, 0:nk], rhs=qT[:, h, i * 128:i * 128 + ts],
                                                        start=True, stop=True), r=[kT[slk], qT], w=[pss[kt_]], inc=(hh == 3))
                    nqh = 2 if ts == 128 else 1
                    for kt_, nk in ((0, 128), (1, ts)):
                        for qh in range(nqh):
                            qw = 64 if nqh == 2 else ts
                            bias = None
                            if nqh == 2:
                                if kt_ == 0:
                                    if first_main and i == 0:
                                        bias = maskc[:, qh:qh + 1]
                                    elif qh == 1:
                                        bias = maskc[:, 2:3]
                                elif qh == 0:
                                    bias = maskc[:, 3:4] if nreal is None else maskc[:, 4:5]
                            src = pss[kt_][0:nk, :].rearrange("p (h q) -> p h q", h=4)[:, :, qh * qw:(qh + 1) * qw]
                            dst = PT[0:nk, :, kt_, qh * qw:(qh + 1) * qw]
                            if bias is None:
                                op("act", lambda e: e.activation(out=dst, in_=src, func=AF.Exp, scale=0.125), r=[pss[kt_]], w=[PT])
                            else:
                                op("act", lambda e: e.activation(out=dst, in_=src, func=AF.Exp, scale=0.125, bias=bias[0:nk, :]), r=[pss[kt_], maskc], w=[PT])
                    po = nb()
                    for hh in range(4):
                        for kt_, slk, nk in ((0, sl_prev, 128), (1, sl_cur, ts)):
                            op("pe", lambda e: e.matmul(po[0:ts, hh * 65:hh * 65 + 65], lhsT=PT[0:nk, hh, kt_, 0:ts], rhs=Vt[slk][0:nk, g, :],
                                                        start=(kt_ == 0), stop=(kt_ == 1)), r=[PT, Vt[slk]], w=[po], inc=(hh == 3 and kt_ == 1))
                    return po

                def attn_back(i, g, po):
                    pov = po[0:ts, 0:260].rearrange("p (h d) -> p h d", h=4)
                    op("dve", lambda e: e.tensor_tensor(out=dn[0:ts, 4 * g:4 * g + 4], in0=pov[:, :, 64], in1=ES[0:ts, 4 * g:4 * g + 4], op=ALU.add), r=[po, ES], w=[dn])
                    op("dve", lambda e: e.reciprocal(out=dn[0:ts, 4 * g:4 * g + 4], in_=dn[0:ts, 4 * g:4 * g + 4]), r=[dn], w=[dn])
                    op("dve", lambda e: e.tensor_tensor(out=attn[0:ts, 256 * g:256 * g + 256].rearrange("p (h d) -> p h d", h=4), in0=pov[:, :, 0:64],
                                                        in1=dn[0:ts, 4 * g:4 * g + 4][:, :, None].broadcast_to([ts, 4, 64]), op=ALU.mult), r=[po, dn], w=[attn])

                def attn_norm(i):
                    op("act", lambda e: e.activation(out=junk[:ts, 0:512], in_=attn[:ts, :], func=AF.Square, accum_out=ss[:ts, 2:3]), r=[attn], w=[junk, ss])
                    rms_rstd(ss[:ts, 2:3], ss[:ts, 2:3], 512, EPS)
                    op("dve", lambda e: e.tensor_scalar(out=hn[:ts, 0:512], in0=attn[:ts, :], scalar1=ss[:ts, 2:3], scalar2=None, op0=ALU.mult), r=[attn, ss], w=[hn])
                    pb = nb()
                    pv = pb[:].bitcast(BF16)
                    for kt in range(4):
                        op("pe", lambda e: e.transpose(out=pv[:, kt * 128:kt * 128 + ts], in_=hn[:ts, kt * 128:(kt + 1) * 128], identity=ident_b[:ts, :ts]),
                           r=[hn, ident_b], w=[pb], inc=(kt == 3))
                    op("act", lambda e: e.copy(out=mT[:, 0:4, i * 128:i * 128 + ts], in_=pv[:, 0:512].rearrange("p (k t) -> p k t", k=4)[:, :, 0:ts]), r=[pb], w=[mT])
                ring[0] = slots[-1]
            else:
                yield
            nseg = (Tn + SEG - 1) // SEG
            sl = min(SEG, Tn)
            lastc = (sl if nreal is None else nreal) - 1
            py = None
            assert nseg == 1
            c1a = Ec[:, :, 1]; s1a = Es[:, :, 1]
            op("dve", lambda e: e.tensor_tensor(out=G6[:, 0, :], in0=Hc[:, 0, :], in1=c1a, op=ALU.mult), r=[Hc, Ec], w=[G6])
            op("dve", lambda e: e.tensor_tensor(out=G6[:, 1, :], in0=Hc[:, 1, :], in1=s1a, op=ALU.mult), r=[Hc, Es, G6], w=[G6])
            op("dve", lambda e: e.tensor_tensor(out=G6[:, 2, :], in0=Hc[:, 0, :], in1=s1a, op=ALU.mult), r=[Hc, Es, G6], w=[G6])
            op("dve", lambda e: e.tensor_tensor(out=G6[:, 3, :], in0=Hc[:, 1, :], in1=c1a, op=ALU.mult), r=[Hc, Ec, G6], w=[G6])
            op("dve", lambda e: e.tensor_tensor(out=G6[:, 4, :], in0=G6[:, 0, :], in1=G6[:, 1, :], op=ALU.subtract), r=[G6], w=[G6])
            op("dve", lambda e: e.tensor_tensor(out=G6[:, 5, :], in0=G6[:, 2, :], in1=G6[:, 3, :], op=ALU.add), r=[G6], w=[G6])
            pyb = {}

            def ssm_front(hg):
                par = hg % 2
                c = hg // 2
                q0 = 2 * hg
                b0, b1, t0, t1, t2, t3 = SB0[par], SB1[par], ST0[par], ST1[par], ST2[par], ST3[par]
                pre_ = nb(); pim = nb()
                for pl in range(2):
                    q = q0 + pl
                    op("pe", lambda e: e.matmul(pre_[:, pl * Tn:(pl + 1) * Tn], lhsT=Bz[:, q, 0, :], rhs=uT[:, c, 0:Tn], start=True, stop=True), r=[Bz, uT], w=[pre_], inc=False)
                    op("pe", lambda e: e.matmul(pim[:, pl * Tn:(pl + 1) * Tn], lhsT=Bz[:, q, 1, :], rhs=uT[:, c, 0:Tn], start=True, stop=True), r=[Bz, uT], w=[pim], inc=(pl == 1))
                pre4 = pre_[:, 0:2 * Tn].rearrange("p (a s l) -> p a s l", a=2, s=nseg)
                pim4 = pim[:, 0:2 * Tn].rearrange("p (a s l) -> p a s l", a=2, s=nseg)

                def v4(bt):
                    return bt[:, :, 0:Tn].rearrange("p a (s l) -> p a s l", s=nseg)
                ecb = Ec[:, q0:q0 + 2, 0:sl][:, :, None, :].broadcast_to([128, 2, nseg, sl])
                esb = Es[:, q0:q0 + 2, 0:sl][:, :, None, :].broadcast_to([128, 2, nseg, sl])
                op("dve", lambda e: e.tensor_tensor(out=v4(t0), in0=pre4, in1=ecb, op=ALU.mult), r=[pre_, Ec], w=[t0])
                op("dve", lambda e: e.tensor_tensor(out=v4(t1), in0=pim4, in1=esb, op=ALU.mult), r=[pim, Es], w=[t1])
                op("dve", lambda e: e.tensor_tensor(out=v4(t2), in0=pim4, in1=ecb, op=ALU.mult), r=[pim, Ec], w=[t2])
                op("dve", lambda e: e.tensor_tensor(out=v4(t3), in0=pre4, in1=esb, op=ALU.mult), r=[pre_, Es], w=[t3])
                op("dve", lambda e: e.tensor_tensor(out=v4(t0), in0=v4(t0), in1=v4(t1), op=ALU.add), r=[t0, t1], w=[t0])
                op("dve", lambda e: e.tensor_tensor(out=v4(t2), in0=v4(t2), in1=v4(t3), op=ALU.subtract), r=[t2, t3], w=[t2])
                for pl in range(2):
                    q = q0 + pl
                    rb = rmag[:, q:q + 1].broadcast_to([128, sl])
                    sre = t0[:, pl, 0:sl]
                    sim_ = t2[:, pl, 0:sl]
                    op("dve", lambda e: e.tensor_tensor_scan(out=sre, data0=rb, data1=sre, initial=G6[:, 4, q:q + 1], op0=ALU.mult, op1=ALU.add), r=[t0, rmag, G6], w=[t0])
                    op("dve", lambda e: e.tensor_tensor_scan(out=sim_, data0=rb, data1=sim_, initial=G6[:, 5, q:q + 1], op0=ALU.mult, op1=ALU.add), r=[t2, rmag, G6], w=[t2])
                op("pool", lambda e: e.tensor_copy(out=GL[:, 0, q0:q0 + 2], in_=t0[:, :, lastc]), r=[t0, GL], w=[GL])
                op("pool", lambda e: e.tensor_copy(out=GL[:, 1, q0:q0 + 2], in_=t2[:, :, lastc]), r=[t2, GL], w=[GL])
                if mode == "pre":
                    return
                op("dve", lambda e: e.tensor_tensor(out=v4(t1), in0=v4(t0), in1=ecb, op=ALU.mult), r=[t0, Ec], w=[t1])
                op("dve", lambda e: e.tensor_tensor(out=v4(t3), in0=v4(t2), in1=esb, op=ALU.mult), r=[t2, Es], w=[t3])
                op("dve", lambda e: e.tensor_tensor(out=v4(b0), in0=v4(t1), in1=v4(t3), op=ALU.subtract), r=[t1, t3], w=[b0])
                op("dve", lambda e: e.tensor_tensor(out=v4(b1), in0=v4(t2), in1=ecb, op=ALU.mult), r=[t2, Ec], w=[b1])
                op("dve", lambda e: e.tensor_tensor(out=v4(sc_ab), in0=v4(t0), in1=esb, op=ALU.mult), r=[t0, Es], w=[sc_ab])
                op("dve", lambda e: e.tensor_tensor(out=v4(b1), in0=v4(b1), in1=v4(sc_ab), op=ALU.add), r=[b1, sc_ab], w=[b1])
                hbr, hbi = HBR[par], HBI[par]
                op("act", lambda e: e.copy(out=hbr[:, :, 0:Tn], in_=b0[:, :, 0:Tn]), r=[b0], w=[hbr])
                op("act", lambda e: e.copy(out=hbi[:, :, 0:Tn], in_=b1[:, :, 0:Tn]), r=[b1], w=[hbi])

            def ssm_back(hg):
                par = hg % 2
                c = hg // 2
                q0 = 2 * hg
                hbr, hbi = HBR[par], HBI[par]
                py = pyb.get(c)
                for pl in range(2):
                    q = q0 + pl
                    qm = q % 4
                    if qm == 0:
                        py = nb()
                        pyb[c] = py
                    op("pe", lambda e: e.matmul(py[32 * qm:32 * qm + 32, 0:Tn], lhsT=Cz[:, q, 0, :], rhs=hbr[:, pl, 0:Tn], start=True, stop=False,
                                                tile_position=(0, 32 * qm)), r=[Cz, hbr], w=[py], inc=False)
                    op("pe", lambda e: e.matmul(py[32 * qm:32 * qm + 32, 0:Tn], lhsT=Cz[:, q, 1, :], rhs=hbi[:, pl, 0:Tn], start=False, stop=True,
                                                tile_position=(0, 32 * qm)), r=[Cz, hbi], w=[py])
                if hg % 2 == 1:
                    w1 = W1[c % 2]; w2 = W2[c % 2]
                    yv = w1
                    op("dve", lambda e: e.scalar_tensor_tensor(out=yv[:, 0:Tn], in0=uT[:, c, 0:Tn], scalar=Dcol[:, c:c + 1], in1=py[:, 0:Tn], op0=ALU.mult, op1=ALU.add),
                       r=[uT, Dcol, py], w=[w1])
                    op("act", lambda e: e.activation(out=w2[:, 0:Tn], in_=yv[:, 0:Tn], func=AF.Square, scale=math.sqrt(0.044715)), r=[w1], w=[w2])
                    op("act", lambda e: e.mul(out=zf[:, c, 0:Tn], in_=yv[:, 0:Tn], mul=0.5), r=[w1], w=[zf])
                    op("dve", lambda e: e.scalar_tensor_tensor(out=w2[:, 0:Tn], in0=w2[:, 0:Tn], scalar=1.0, in1=yv[:, 0:Tn], op0=ALU.add, op1=ALU.mult), r=[w2, w1], w=[w2])
                    op("act", lambda e: e.activation(out=w2[:, 0:Tn], in_=w2[:, 0:Tn], func=AF.Tanh, scale=0.7978845608028654), r=[w2], w=[w2])
                    op("dve", lambda e: e.scalar_tensor_tensor(out=zf[:, c, 0:Tn], in0=w2[:, 0:Tn], scalar=1.0, in1=zf[:, c, 0:Tn], op0=ALU.add, op1=ALU.mult), r=[w2, zf], w=[zf])
                    op("pool", lambda e: e.tensor_copy(out=zb[:, c, 0:Tn], in_=zf[:, c, 0:Tn]), r=[zf], w=[zb])

            units = [(i_, g_) for i_ in range(nt) for g_ in range(2)] if mode != "pre" else []
            ui = 0
            pend = None
            for hg in range(8):
                if pend is not None:
                    attn_back(*pend)
                    if pend[1] == 1:
                        attn_norm(pend[0])
                    pend = None
                if mode != "pre" and hg % 2 == 1 and ui < len(units):
                    pend = (units[ui][0], units[ui][1], attn_unit(*units[ui]))
                    ui += 1
                ssm_front(hg)
                if mode == "pre":
                    continue
                if hg >= 1:
                    ssm_back(hg - 1)
            if mode != "pre":
                if pend is not None:
                    attn_back(*pend)
                    if pend[1] == 1:
                        attn_norm(pend[0])
                    pend = None
                ssm_back(7)
                while ui < len(units):
                    po_ = attn_unit(*units[ui])
                    attn_back(units[ui][0], units[ui][1], po_)
                    if units[ui][1] == 1:
                        attn_norm(units[ui][0])
                    ui += 1
            w2 = W2[0]
            cLa = Ec[:, :, lastc]; sLa = Es[:, :, lastc]
            op("dve", lambda e: e.tensor_tensor(out=G6[:, 0, :], in0=GL[:, 0, :], in1=cLa, op=ALU.mult), r=[GL, Ec, G6], w=[G6])
            op("dve", lambda e: e.tensor_tensor(out=G6[:, 1, :], in0=GL[:, 1, :], in1=sLa, op=ALU.mult), r=[GL, Es, G6], w=[G6])
            op("dve", lambda e: e.tensor_tensor(out=G6[:, 2, :], in0=GL[:, 1, :], in1=cLa, op=ALU.mult), r=[GL, Ec, G6], w=[G6])
            op("dve", lambda e: e.tensor_tensor(out=G6[:, 3, :], in0=GL[:, 0, :], in1=sLa, op=ALU.mult), r=[GL, Es, G6], w=[G6])
            op("dve", lambda e: e.tensor_tensor(out=Hc[:, 0, :], in0=G6[:, 0, :], in1=G6[:, 1, :], op=ALU.subtract), r=[G6, Hc], w=[Hc])
            op("dve", lambda e: e.tensor_tensor(out=Hc[:, 1, :], in0=G6[:, 2, :], in1=G6[:, 3, :], op=ALU.add), r=[G6, Hc], w=[Hc])
            if mode == "pre":
                return
            for c2 in range(4):
                pg = nb()
                for c in range(4):
                    op("pe", lambda e: e.matmul(pg[:, 0:Tn], lhsT=Wglu[:, c, c2 * 128:(c2 + 1) * 128], rhs=zb[:, c, 0:Tn], start=(c == 0), stop=(c == 3)),
                       r=[Wglu, zb], w=[pg], inc=(c == 3))
                op("act", lambda e: e.activation(out=w2[:, 0:Tn], in_=pg[:, 0:Tn], func=AF.Tanh, scale=0.5), r=[pg], w=[w2])
                op("dve", lambda e: e.scalar_tensor_tensor(out=s2[:, c2, 0:Tn], in0=w2[:, 0:Tn], scalar=1.0, in1=zf[:, c2, 0:Tn], op0=ALU.add, op1=ALU.mult),
                   r=[w2, zf], w=[s2])
                op("act", lambda e: e.activation(out=sqb[:, c2, 0:Tn], in_=s2[:, c2, 0:Tn], func=AF.Square), r=[s2], w=[sqb])
            pn = nb()
            for c in range(4):
                op("pe", lambda e: e.matmul(pn[:, 0:Tn], lhsT=ones_b[:], rhs=sqb[:, c, 0:Tn], start=(c == 0), stop=(c == 3)), r=[ones_b, sqb], w=[pn], inc=(c == 3))
            op("act", lambda e: e.activation(out=rstd_s[:, 0:Tn], in_=pn[:, 0:Tn], func=AF.Ln, scale=1.0 / 512, bias=eps4_c[:, :]), r=[pn, eps4_c], w=[rstd_s])
            op("act", lambda e: e.activation(out=rstd_s[:, 0:Tn], in_=rstd_s[:, 0:Tn], func=AF.Exp, scale=-0.5), r=[rstd_s], w=[rstd_s])
            for c in range(4):
                op("pool", lambda e: e.tensor_tensor(out=mT[:, 4 + c, 0:Tn], in0=s2[:, c, 0:Tn], in1=rstd_s[:, 0:Tn], op=ALU.mult), r=[s2, rstd_s], w=[mT])
            for i in range(nt):
                for hf in range(2):
                    pb = nb()
                    for kt in range(8):
                        op("pe", lambda e: e.matmul(pb[0:ts, :], lhsT=mT[:, kt, i * 128:i * 128 + ts], rhs=Wout[:, kt, hf * 512:(hf + 1) * 512], start=(kt == 0), stop=(kt == 7)),
                           r=[mT, Wout], w=[pb], inc=(kt == 7))
                    op("dve", lambda e: e.tensor_tensor(out=Xb[0:ts, i, hf * 512:(hf + 1) * 512], in0=Xb[0:ts, i, hf * 512:(hf + 1) * 512], in1=pb[0:ts, :], op=ALU.add),
                       r=[Xb, pb], w=[Xb])
            norm_transpose(Xb, nt, ts, xT2)
            halo = (mode == "halo")
            if not halo:
                for j_ in range(NWD):
                    dn_issue(j_)
            issue_rope(nx, bi + 1)
            tcol = Tn if nreal is None else nreal
            for j in range(NHID):
                wb = up_get()
                pgv = []
                for gv in range(2):
                    pb = nb()
                    for kt in range(8):
                        op("pe", lambda e: e.matmul(pb[:, 0:Tn], lhsT=wb[:, kt, gv * 128:(gv + 1) * 128], rhs=xT2[:, kt, 0:Tn], start=(kt == 0), stop=(kt == 7)),
                           r=[wb, xT2], w=[pb], inc=(kt == 7))
                    pgv.append(pb)
                if j + NWU < NHID:
                    up_issue(j + NWU)
                cg, cv, th = CGS[j % 2], CVS[j % 2], THS[j % 2]
                outs = (cg, cv)
                for gv in range(2):
                    hc = gv * NHID + j
                    u_ = UPS[j % 2][gv]
                    op("pool", lambda e: e.tensor_copy(out=u_[:, 0:2], in_=TAIL[:, hc, :]), r=[TAIL], w=[u_])
                    op("act", lambda e: e.copy(out=u_[:, 2:2 + Tn], in_=pgv[gv][:, 0:Tn]), r=[pgv[gv], u_], w=[u_])
                    op("pool", lambda e: e.tensor_copy(out=TAIL[:, hc, :], in_=u_[:, tcol:tcol + 2]), r=[u_, TAIL], w=[TAIL])
                    if halo:
                        continue
                    o_ = outs[gv]
                    eng = "dve"
                    op("act", lambda e: e.activation(out=o_[:, 0:Tn], in_=u_[:, 2:2 + Tn], func=AF.Identity, scale=cw[:, 2, hc:hc + 1], bias=cb[:, hc:hc + 1]),
                       r=[u_, cw, cb], w=[o_])
                    op(eng, lambda e: e.scalar_tensor_tensor(out=o_[:, 0:Tn], in0=u_[:, 1:1 + Tn], scalar=cw[:, 1, hc:hc + 1], in1=o_[:, 0:Tn], op0=ALU.mult, op1=ALU.add),
                       r=[u_, cw, o_], w=[o_])
                    op(eng, lambda e: e.scalar_tensor_tensor(out=o_[:, 0:Tn], in0=u_[:, 0:Tn], scalar=cw[:, 0, hc:hc + 1], in1=o_[:, 0:Tn], op0=ALU.mult, op1=ALU.add),
                       r=[u_, cw, o_], w=[o_])
                if halo:
                    continue
                op("act", lambda e: e.activation(out=th[:, 0:Tn], in_=cg[:, 0:Tn], func=AF.Tanh, scale=0.5), r=[cg], w=[th])
                op("dve", lambda e: e.tensor_tensor(out=cv[:, 0:Tn], in0=cg[:, 0:Tn], in1=cv[:, 0:Tn], op=ALU.mult), r=[cg, cv], w=[cv])
                op("dve", lambda e: e.scalar_tensor_tensor(out=actT[:, j, 0:Tn], in0=th[:, 0:Tn], scalar=1.0, in1=cv[:, 0:Tn], op0=ALU.add, op1=ALU.mult),
                   r=[th, cv], w=[actT])
            if halo:
                return
            yield
            accs = [nb() for _ in range(nt * 2)]
            for j in range(NHID):
                wd = dn_get()
                for i in range(nt):
                    for hf in range(2):
                        pb = accs[i * 2 + hf]
                        op("pe", lambda e: e.matmul(pb[0:ts, :], lhsT=actT[:, j, i * 128:i * 128 + ts], rhs=wd[:, hf * 512:(hf + 1) * 512], start=(j == 0), stop=(j == NHID - 1)),
                           r=[actT, wd], w=[pb], inc=(j == NHID - 1 or (i == nt - 1 and hf == 1)))
                if j + NWD < NHID:
                    dn_issue(j + NWD)
            for i in range(nt):
                for hf in range(2):
                    pb = accs[i * 2 + hf]
                    op("dve", lambda e: e.tensor_tensor(out=Xb[0:ts, i, hf * 512:(hf + 1) * 512], in0=Xb[0:ts, i, hf * 512:(hf + 1) * 512], in1=pb[0:ts, :], op=ALU.add),
                       r=[Xb, pb], w=[Xb])
                op("act", lambda e: e.activation(out=junk[:ts, :], in_=Xb[:ts, i, :], func=AF.Square, accum_out=ss[:ts, 3:4]), r=[Xb], w=[junk, ss])
                rms_rstd(ss[:ts, 3:4], ss[:ts, 3:4], D, EPS)
                op("dve", lambda e: e.scalar_tensor_tensor(out=Xb[0:ts, i, :], in0=Xb[0:ts, i, :], scalar=ss[:ts, 3:4], in1=GF[0:ts, :], op0=ALU.mult, op1=ALU.mult),
                   r=[Xb, ss, GF], w=[Xb])
            if nreal is None:
                dma(ydst.rearrange("(n p) d -> p n d", p=ts), Xb[0:ts, 0:nt, :], r=[Xb], owner=Yown[bi % 2])
            else:
                dma(ydst, Xb[0:nr, 0, :], r=[Xb], owner=Yown[bi % 2])
            out_owners.append(Yown[bi % 2])
            if os.environ.get("KDBG") and ydst is not None and samp is None:
                dma(ydst[0:128, 0:256], zf[:, 0, :], r=[zf], owner=Yown[bi % 2])
                dma(ydst[128:256, 0:512], attn[:, :], r=[attn], owner=Yown[bi % 2])
                dbg_t = sb([128, 256], F32, "dbg_t")
                op("dve", lambda e: e.tensor_copy(out=dbg_t[:], in_=mT[:, 5, :]), r=[mT], w=[dbg_t])
                dma(ydst[0:128, 256:512], dbg_t[:], r=[dbg_t], owner=Yown[bi % 2])
                dbg_u = sb([128, 256], F32, "dbg_u")
                op("dve", lambda e: e.tensor_copy(out=dbg_u[:], in_=HBR[1][:, 1, :]), r=[HBR[1]], w=[dbg_u])
                dma(ydst[0:128, 512:768], dbg_u[:], r=[dbg_u], owner=Yown[bi % 2])
            if samp is not None or last:
                op("act", lambda e: e.copy(out=Ho[:], in_=Hc[:]), r=[Hc], w=[Ho])
                op("act", lambda e: e.copy(out=To[:], in_=TAIL[:]), r=[TAIL], w=[To])
                d_re = sres_out[samp] if samp is not None else sre_out
                d_im = sims_out[samp] if samp is not None else sim_out
                d_cv = convs_out[samp] if samp is not None else conv_out
                dma(d_re.rearrange("(q gl) p -> (gl p) q", gl=2), Ho[:, 0, :], r=[Ho], owner=Ho, slow=True)
                dma(d_im.rearrange("(q gl) p -> (gl p) q", gl=2), Ho[:, 1, :], r=[Ho], owner=Ho, slow=True)
                for r_ in range(2):
                    dma(d_cv[r_].rearrange("(c p) -> p c", p=128), To[:, :, r_], r=[To], owner=To, slow=True)
                out_owners.extend([Ho, To])

        specs = []
        for b in range(n_pre if KSTOP >= 2 else 0):
            specs.append(dict(xsrc=xprev[b * T:(b + 1) * T, :], Tn=T, mode="pre"))
        if KSTOP >= 3:
            specs.append(dict(xsrc=xprev[NP:NP + T, :], Tn=T, mode="halo", rope_off=0))
        for b in range(n_main if KSTOP >= 4 else 0):
            specs.append(dict(xsrc=xmain[b * T:(b + 1) * T, :], Tn=T, mode="full", rope_off=T + b * T, first_main=(b == 0),
                              ydst=y_main[b * T:(b + 1) * T, :], last=(b == n_main - 1)))
        if do_sample and KSTOP >= 5:
            for s_ in range(2):
                specs.append(dict(xsrc=xs[s_], Tn=128, mode="full", rope_off=T + NM + 128 * s_, ydst=y_s[s_], samp=s_, nreal=16))
        nsp = len(specs)
        gens = []
        for si, sp_ in enumerate(specs):
            kw = {k_: v_ for k_, v_ in sp_.items() if k_ not in ("x_loaded", "rope_loaded")}
            gens.append(block(si, sp_, specs[si + 1] if si + 1 < nsp else None, **kw))
        for si in range(min(2, nsp)):
            issue_x(specs[si], si)
        if nsp:
            next(gens[0], None)
        for si in range(nsp):
            next(gens[si], None)
            if si + 1 < nsp:
                next(gens[si + 1], None)
            next(gens[si], None)
            if si + 2 < nsp:
                issue_x(specs[si + 2], si + 2)
        seen = set()
        for ob in out_owners:
            if id(ob) in seen or ob.k.dsem is None:
                continue
            seen.add(id(ob))
            nc.sync.wait_ge(ob.k.dsem, ob.k.dcnt)
    return nc


_NC_CACHE = {}


def _rope_tables(pos):
    half = 32
    inv = 10000.0 ** (-np.arange(half, dtype=np.float64) / half)
    ang = (pos.astype(np.float32)[None, :] * inv.astype(np.float32)[:, None]).astype(np.float32)
    c = np.cos(ang.astype(np.float64)).astype(np.float32)
    s = np.sin(ang.astype(np.float64)).astype(np.float32)
    return np.concatenate([c, c], axis=0), np.concatenate([s, s], axis=0)


def kernel(x_prompt, x_sample, cache_k, cache_v, state_ssm_re, state_ssm_im, state_conv,
           norm1_g, w_in, attn_sinks, ssm_A_re, ssm_A_im, ssm_log_dt, ssm_B_re, ssm_B_im,
           ssm_C_re, ssm_C_im, ssm_D, w_glu, onorm_attn_g, onorm_ssm_g, w_out, norm2_g,
           w_up, conv_w, conv_b, w_down, final_g):
    f = lambda a: np.ascontiguousarray(np.asarray(a, dtype=np.float32))
    NM = 4096
    n_main = NM // T
    n_pre = (4096 - T) // T
    key = (n_pre, n_main)
    if key not in _NC_CACHE:
        _NC_CACHE[key] = build(n_pre, n_main)
    nc = _NC_CACHE[key]
    shared = dict(
        norm1_g=f(norm1_g[0]), w_in=f(w_in[0]), sinks=f(attn_sinks[0]), A_re=f(ssm_A_re[0]), A_im=f(ssm_A_im[0]),
        log_dt=f(ssm_log_dt[0]), B_re=f(ssm_B_re[0]), B_im=f(ssm_B_im[0]), C_re=f(ssm_C_re[0]), C_im=f(ssm_C_im[0]),
        ssm_D=f(ssm_D[0]), w_glu=f(w_glu[0]), on_a=f(onorm_attn_g[0]), on_s=f(onorm_ssm_g[0]), w_out=f(w_out[0]),
        norm2_g=f(norm2_g[0]), w_up=f(w_up[0]), conv_w=f(conv_w[0]), conv_b=f(conv_b[0]), w_down=f(w_down[0]),
        final_g=f(final_g), ident=np.eye(128, dtype=np.float32))
    half_first = np.zeros(128, np.float32); half_first[:64] = NEG
    half_second = np.zeros(128, np.float32); half_second[64:] = NEG
    full = np.full(128, NEG, np.float32)
    samp_m = np.zeros(128, np.float32); samp_m[16:] = NEG
    in_maps = []
    for c in range(8):
        seq, hf = c // 2, c % 2
        m = dict(shared)
        if hf == 0:
            m["xprev"] = np.zeros((4096, D), np.float32)
            m["maskc"] = np.stack([full, full, half_first, half_second, samp_m], axis=1)
            base = 0
        else:
            m["xprev"] = f(x_prompt[seq, 0:4096])
            m["maskc"] = np.stack([np.zeros(128, np.float32), half_first, half_first, half_second, samp_m], axis=1)
            base = 4096
        m["xmain"] = f(x_prompt[seq, hf * 4096:(hf + 1) * 4096])
        pos = np.concatenate([np.arange(base - T, base + NM), 2048 + np.arange(128), 2048 + np.arange(128)]).astype(np.float64)
        pos = np.maximum(pos, 0)
        rc_, rs_ = _rope_tables(pos)
        m["ropec"] = f(rc_); m["ropes"] = f(rs_)
        m["xs"] = f(x_sample[2 * c:2 * c + 2])
        m["cache_k"] = f(cache_k[0, 2 * c:2 * c + 2]).reshape(2, 128, 128)
        m["cache_v"] = f(cache_v[0, 2 * c:2 * c + 2]).reshape(2, 128, 128)
        m["st_re"] = f(state_ssm_re[0, 2 * c:2 * c + 2]); m["st_im"] = f(state_ssm_im[0, 2 * c:2 * c + 2])
        m["st_conv"] = f(state_conv[0, 2 * c:2 * c + 2])
        in_maps.append(m)
    res = run_bass_kernel_spmd(nc, in_maps, core_ids=list(range(8))).results
    y_prompt = np.zeros((4, 8192, D), np.float32)
    kp = np.zeros((1, 4, 128, 2, 64), np.float32); vp = np.zeros_like(kp)
    rp = np.zeros((1, 4, 32, 64), np.float32); ip = np.zeros_like(rp)
    cp = np.zeros((1, 4, 2, 5632), np.float32)
    ys = np.zeros((16, 16, D), np.float32)
    ksm = np.zeros((1, 16, 128, 2, 64), np.float32); vsm = np.zeros_like(ksm)
    rsm = np.zeros((1, 16, 32, 64), np.float32); ism = np.zeros_like(rsm)
    csm = np.zeros((1, 16, 2, 5632), np.float32)
    for c in range(8):
        r = res[c]
        seq, hf = c // 2, c % 2
        y_prompt[seq, hf * 4096:(hf + 1) * 4096] = r["y_main"]
        if hf == 1:
            kp[0, seq] = r["k_out"].reshape(128, 2, 64); vp[0, seq] = r["v_out"].reshape(128, 2, 64)
            rp[0, seq] = r["sre_out"]; ip[0, seq] = r["sim_out"]; cp[0, seq] = r["conv_out"]
        ys[2 * c:2 * c + 2] = r["y_s"]
        ksm[0, 2 * c:2 * c + 2] = r["ks_out"].reshape(2, 128, 2, 64); vsm[0, 2 * c:2 * c + 2] = r["vs_out"].reshape(2, 128, 2, 64)
        rsm[0, 2 * c:2 * c + 2] = r["sres_out"]; ism[0, 2 * c:2 * c + 2] = r["sims_out"]; csm[0, 2 * c:2 * c + 2] = r["convs_out"]
    return (y_prompt, ys, kp, vp, rp, ip, cp, ksm, vsm, rsm, ism, csm)
```
